# Optimizing a Trainium2 kernel written in Bass

```python
import math
import jax, jax.numpy as jnp
from jax import lax
import numpy as np

D_MODEL = 2048
BATCH = 8
SEQ = 2048
DEPTH = 2

GRID_W = 64
CTX_LEN = 256
EPS = 1e-6
NA_HEADS = 8
NA_HEAD_DIM = 128
NA_WIDTH = NA_HEADS * NA_HEAD_DIM
NA_WIN_R = 8
NA_WIN_C = 16
NA_QBLK = 16
NA_KBLK = 32
FOUR_GROUPS = 4
FOUR_GROUP_DIM = 256
FOUR_WIDTH = FOUR_GROUPS * FOUR_GROUP_DIM
SSD_HEADS = 16
SSD_HEAD_DIM = 64
SSD_WIDTH = SSD_HEADS * SSD_HEAD_DIM
SSD_GROUPS = 4
SSD_HPG = SSD_HEADS // SSD_GROUPS
SSD_STATE = 128
SSD_CONV = 7
SSD_CHUNK = 128
SSD_CONV_CH = SSD_WIDTH + 2 * SSD_GROUPS * SSD_STATE
ROPE_BASE = 10000.0
IN_SPLITS = (NA_WIDTH, NA_WIDTH, NA_WIDTH, NA_WIDTH, FOUR_WIDTH, FOUR_WIDTH,
             SSD_CONV_CH, SSD_WIDTH, 2 * SSD_HEADS, D_MODEL, D_MODEL, D_MODEL)
IN_WIDTH = sum(IN_SPLITS)

kernel_name = "hybrid_natten_fnet_ssd_prefix_block"


def rmsnorm(x, w):
    xf = x.astype(jnp.float32)
    y = xf * lax.rsqrt(jnp.mean(xf * xf, axis=-1, keepdims=True) + EPS)
    return (y * w.astype(jnp.float32)).astype(x.dtype)


def split_cols(p):
    offs = np.cumsum(IN_SPLITS)[:-1].tolist()
    return jnp.split(p, offs, axis=-1)


def axial_rope(t):
    L, n = t.shape[1], t.shape[-1]
    quarter = n // 4
    pos = jnp.arange(L)
    inv = ROPE_BASE ** (-jnp.arange(quarter, dtype=jnp.float32) / quarter)

    def rot(u, p):
        ang = p.astype(jnp.float32)[:, None] * inv
        cos = jnp.cos(ang)[None, :, None, :].astype(u.dtype)
        sin = jnp.sin(ang)[None, :, None, :].astype(u.dtype)
        u1, u2 = u[..., :quarter], u[..., quarter:]
        return jnp.concatenate([u1 * cos - u2 * sin, u1 * sin + u2 * cos], axis=-1)

    return jnp.concatenate([rot(t[..., :2 * quarter], pos // GRID_W),
                            rot(t[..., 2 * quarter:], pos % GRID_W)], axis=-1)


def dwconv(u, w, b):
    k = w.shape[0]
    y = lax.conv_general_dilated(u, w[:, None, :], window_strides=(1,), padding=[(k // 2, k // 2)],
                                 dimension_numbers=("NWC", "WIO", "NWC"), feature_group_count=u.shape[-1])
    return y + b


def segsum(a):
    t = a.shape[-1]
    cs = jnp.cumsum(a, axis=-1)
    d = cs[..., :, None] - cs[..., None, :]
    return jnp.where(jnp.tril(jnp.ones((t, t), dtype=bool)), d, -jnp.inf)


def ssd_chunked(xdt, a, bm, cm, init):
    b, L, g, k, p = xdt.shape
    nc = L // SSD_CHUNK
    x = xdt.reshape(b, nc, SSD_CHUNK, g, k, p)
    A = a.astype(jnp.float32).reshape(b, nc, SSD_CHUNK, g, k).transpose(0, 3, 4, 1, 2)
    Bc = bm.reshape(b, nc, SSD_CHUNK, g, -1)
    Cc = cm.reshape(b, nc, SSD_CHUNK, g, -1)
    a_cs = jnp.cumsum(A, axis=-1)
    lmat = jnp.exp(segsum(A))
    y_diag = jnp.einsum("bclgn,bcsgn,bgkcls,bcsgkp->bclgkp", Cc, Bc, lmat, x)
    decay_states = jnp.exp(a_cs[..., -1:] - a_cs)
    states = jnp.einsum("bclgn,bgkcl,bclgkp->bcgkpn", Bc, decay_states, x)
    states = jnp.concatenate([init[:, None].astype(states.dtype), states], axis=1)
    chunk_decay = jnp.exp(segsum(jnp.pad(a_cs[..., -1], ((0, 0), (0, 0), (0, 0), (1, 0)))))
    states = jnp.einsum("bgkzc,bcgkpn->bzgkpn", chunk_decay, states)
    y_off = jnp.einsum("bclgn,bcgkpn,bgkcl->bclgkp", Cc, states[:, :-1], jnp.exp(a_cs))
    y = (y_diag + y_off).reshape(b, L, g, k, p)
    return y, states[:, -1]


def ssd_mixer(xbc_c, dt_c, xbc_l, dt_l, conv_w, conv_b, dt_bias, a_log, d_skip):
    out_dtype = xbc_l.dtype

    def prep(xbc, dt_raw, rope):
        u = jax.nn.silu(dwconv(xbc, conv_w, conv_b))
        xs, bm, cm = jnp.split(u, [SSD_WIDTH, SSD_WIDTH + SSD_GROUPS * SSD_STATE], axis=-1)
        b, L = xs.shape[:2]
        xs = xs.reshape(b, L, SSD_GROUPS, SSD_HPG, SSD_HEAD_DIM)
        bm = bm.reshape(b, L, SSD_GROUPS, SSD_STATE)
        cm = cm.reshape(b, L, SSD_GROUPS, SSD_STATE)
        if rope:
            bm, cm = axial_rope(bm), axial_rope(cm)
        dt = jax.nn.softplus(dt_raw.astype(jnp.float32).reshape(b, L, 2, SSD_GROUPS, SSD_HPG)
                             + dt_bias.astype(jnp.float32).reshape(2, SSD_GROUPS, SSD_HPG))
        return xs, bm, cm, dt

    xc, bc, cc, dtc = prep(xbc_c, dt_c, False)
    xl, bl, cl, dtl = prep(xbc_l, dt_l, True)
    a = -jnp.exp(a_log.astype(jnp.float32)).reshape(2, SSD_GROUPS, SSD_HPG)
    dsk = d_skip.astype(jnp.float32).reshape(SSD_GROUPS, SSD_HPG, 1)
    b = xl.shape[0]
    init = jnp.zeros((b, SSD_GROUPS, SSD_HPG, SSD_HEAD_DIM, SSD_STATE), jnp.float32)
    yc = dsk * xc.astype(jnp.float32)
    yl = dsk * xl.astype(jnp.float32)
    for direction in range(2):
        flip = (lambda t: t[:, ::-1]) if direction == 1 else (lambda t: t)

        def run(xs, bm, cm, dt, s0):
            dtd = dt[:, :, direction]
            y, s = ssd_chunked(flip(xs * dtd[..., None]), flip(dtd * a[direction]), flip(bm), flip(cm), s0)
            return flip(y), s

        y_c, s_c = run(xc, bc, cc, dtc, init)
        y_l, _ = run(xl, bl, cl, dtl, s_c)
        yc = yc + y_c
        yl = yl + y_l
    return (yc.reshape(yc.shape[0], yc.shape[1], SSD_WIDTH).astype(out_dtype),
            yl.reshape(b, yl.shape[1], SSD_WIDTH).astype(out_dtype))


def na_latent(q, k, v, k_ctx, v_ctx, rpb):
    b, L, h, d = q.shape
    rows = L // GRID_W
    win_r = min(NA_WIN_R, rows)
    nblk = GRID_W // NA_QBLK
    scale = d ** -0.5
    qcol = np.arange(GRID_W).reshape(nblk, NA_QBLK)
    qstart = np.clip(qcol - NA_WIN_C // 2, 0, GRID_W - NA_WIN_C)
    kstart = np.clip(qcol[:, 0] - NA_WIN_C // 2, 0, GRID_W - NA_KBLK)
    kcol = (kstart[:, None] + np.arange(NA_KBLK)[None]).astype(np.int32)
    col_mask = ((kcol[:, None, :] >= qstart[..., None]) &
                (kcol[:, None, :] < qstart[..., None] + NA_WIN_C))
    dc = np.clip(kcol[:, None, :] - qcol[..., None] + NA_WIN_C - 1, 0, 2 * NA_WIN_C - 2).astype(np.int32)
    kg = k.reshape(b, rows, GRID_W, h, d)
    vg = v.reshape(b, rows, GRID_W, h, d)
    qg = q.reshape(b, rows, nblk, NA_QBLK, h, d).transpose(1, 0, 2, 3, 4, 5)
    rpb32 = rpb.astype(jnp.float32)
    nwin = win_r * NA_KBLK

    def row_block(args):
        r, q_r = args
        start = jnp.clip(r - win_r // 2, 0, rows - win_r)
        k_r = lax.dynamic_slice_in_dim(kg, start, win_r, axis=1)[:, :, kcol]
        v_r = lax.dynamic_slice_in_dim(vg, start, win_r, axis=1)[:, :, kcol]
        dr = start + jnp.arange(win_r) - r + NA_WIN_R - 1
        bias = rpb32[:, dr][:, :, dc].transpose(0, 2, 3, 1, 4)
        s_w = jnp.einsum("bjqhd,bwjkhd->bhjqwk", q_r, k_r).astype(jnp.float32) * scale + bias[None]
        s_w = jnp.where(col_mask[:, :, None, :], s_w, -1e30)
        s_c = jnp.einsum("bjqhd,bchd->bhjqc", q_r, k_ctx).astype(jnp.float32) * scale
        s = jnp.concatenate([s_w.reshape(b, h, nblk, NA_QBLK, nwin), s_c], axis=-1)
        p = jax.nn.softmax(s, axis=-1).astype(v.dtype)
        p_w = p[..., :nwin].reshape(b, h, nblk, NA_QBLK, win_r, NA_KBLK)
        return (jnp.einsum("bhjqwk,bwjkhd->bjqhd", p_w, v_r)
                + jnp.einsum("bhjqc,bchd->bjqhd", p[..., nwin:], v_ctx))

    o = lax.map(row_block, (jnp.arange(rows), qg))
    return o.transpose(1, 0, 2, 3, 4, 5).reshape(b, L, h * d)


def ctx_attention(q, k, v):
    b, n, h, d = q.shape
    s = jnp.einsum("bqhd,bkhd->bhqk", q, k).astype(jnp.float32) * (d ** -0.5)
    p = jax.nn.softmax(s, axis=-1).astype(v.dtype)
    return jnp.einsum("bhqk,bkhd->bqhd", p, v).reshape(b, n, h * d)


def fourier_mix(u, w):
    b, L, _ = u.shape
    ug = u.astype(jnp.float32).reshape(b, L, FOUR_GROUPS, FOUR_GROUP_DIM)
    f = jnp.fft.fft2(ug, axes=(1, 3), norm="ortho").real
    return f.reshape(b, L, FOUR_WIDTH).astype(u.dtype) @ w


def heads(t):
    return t.reshape(t.shape[0], t.shape[1], NA_HEADS, NA_HEAD_DIM)


def layer(xc, xl, c, c_ctx, w_ada, b_ada, norm_w, w_in, rpb, four_w, conv_w, conv_b, dt_bias, a_log, d_skip,
          ssd_norm_w, wb_na, wb_four, wb_ssd, w_out, update_ctx):
    mod_l = jax.nn.silu(c) @ w_ada + b_ada
    mod_c = jax.nn.silu(c_ctx) @ w_ada + b_ada
    sh_l, sc_l, g_l = jnp.split(mod_l[:, None, :], 3, axis=-1)
    sh_c, sc_c, g_c = jnp.split(mod_c, 3, axis=-1)
    hl = rmsnorm(xl, norm_w) * (1 + sc_l) + sh_l
    hc = rmsnorm(xc, norm_w) * (1 + sc_c) + sh_c
    q_l, k_l, v_l, zna_l, uf_l, zf_l, xbc_l, zs_l, dt_l, gna_l, gf_l, gs_l = split_cols(hl @ w_in)
    q_c, k_c, v_c, zna_c, uf_c, zf_c, xbc_c, zs_c, dt_c, gna_c, gf_c, gs_c = split_cols(hc @ w_in)

    def merge(a, za, f, zf, s, zs, ga, gf, gs):
        o_a = (a * jax.nn.silu(za)) @ wb_na
        o_f = (f * jax.nn.silu(zf)) @ wb_four
        o_s = rmsnorm(s * jax.nn.silu(zs), ssd_norm_w) @ wb_ssd
        return (jax.nn.sigmoid(ga) * o_a + jax.nn.sigmoid(gf) * o_f + jax.nn.sigmoid(gs) * o_s) @ w_out

    kc_h, vc_h = heads(k_c), heads(v_c)
    a_l = na_latent(heads(q_l), heads(k_l), heads(v_l), kc_h, vc_h, rpb)
    f_l = fourier_mix(uf_l, four_w)
    s_c, s_l = ssd_mixer(xbc_c, dt_c, xbc_l, dt_l, conv_w, conv_b, dt_bias, a_log, d_skip)
    xl_new = xl + g_l * merge(a_l, zna_l, f_l, zf_l, s_l, zs_l, gna_l, gf_l, gs_l)
    if update_ctx:
        a_c = ctx_attention(heads(q_c), kc_h, vc_h)
        f_c = fourier_mix(uf_c, four_w)
        xc = xc + g_c * merge(a_c, zna_c, f_c, zf_c, s_c, zs_c, gna_c, gf_c, gs_c)
    return xc, xl_new


def setup_inputs(seed: int = 0) -> dict:
    key = jax.random.key(seed)
    ks = jax.random.split(key, 24)

    def nrm(k, shape, s):
        return jax.random.normal(k, shape, jnp.float32) * s

    dt0 = jnp.exp(jax.random.uniform(ks[12], (DEPTH, 2, SSD_HEADS), jnp.float32,
                                     minval=math.log(1e-3), maxval=math.log(1e-1)))
    return {
        "x": nrm(ks[0], (BATCH, SEQ, D_MODEL), 1.0),
        "c": nrm(ks[1], (BATCH, D_MODEL), 1.0),
        "ctx": nrm(ks[2], (BATCH, CTX_LEN, D_MODEL), 1.0),
        "c_ctx": nrm(ks[3], (D_MODEL,), 1.0),
        "w_ada": nrm(ks[4], (DEPTH, D_MODEL, 3 * D_MODEL), 0.5 * D_MODEL ** -0.5),
        "b_ada": nrm(ks[5], (DEPTH, 3 * D_MODEL), 0.02),
        "norm_w": 1.0 + nrm(ks[6], (DEPTH, D_MODEL), 0.05),
        "w_in": nrm(ks[7], (DEPTH, D_MODEL, IN_WIDTH), D_MODEL ** -0.5),
        "na_rpb": nrm(ks[8], (DEPTH, NA_HEADS, 2 * NA_WIN_R - 1, 2 * NA_WIN_C - 1), 0.05),
        "four_w": nrm(ks[9], (DEPTH, FOUR_WIDTH, FOUR_WIDTH), FOUR_WIDTH ** -0.5),
        "ssd_conv_w": nrm(ks[10], (DEPTH, SSD_CONV, SSD_CONV_CH), SSD_CONV ** -0.5),
        "ssd_conv_b": nrm(ks[11], (DEPTH, SSD_CONV_CH), 0.01),
        "ssd_dt_bias": dt0 + jnp.log(-jnp.expm1(-dt0)),
        "ssd_a_log": jnp.log(jax.random.uniform(ks[13], (DEPTH, 2, SSD_HEADS), jnp.float32, minval=1.0, maxval=16.0)),
        "ssd_d": 1.0 + nrm(ks[14], (DEPTH, SSD_HEADS), 0.1),
        "ssd_norm_w": 1.0 + nrm(ks[15], (DEPTH, SSD_WIDTH), 0.05),
        "wb_na": nrm(ks[16], (DEPTH, NA_WIDTH, D_MODEL), NA_WIDTH ** -0.5),
        "wb_four": nrm(ks[17], (DEPTH, FOUR_WIDTH, D_MODEL), FOUR_WIDTH ** -0.5),
        "wb_ssd": nrm(ks[18], (DEPTH, SSD_WIDTH, D_MODEL), SSD_WIDTH ** -0.5),
        "w_out": nrm(ks[19], (DEPTH, D_MODEL, D_MODEL), D_MODEL ** -0.5),
        "final_norm_w": 1.0 + nrm(ks[20], (D_MODEL,), 0.05),
    }


def reference(x, c, ctx, c_ctx, w_ada, b_ada, norm_w, w_in, na_rpb, four_w, ssd_conv_w, ssd_conv_b,
              ssd_dt_bias, ssd_a_log, ssd_d, ssd_norm_w, wb_na, wb_four, wb_ssd, w_out, final_norm_w):
    xc, xl = ctx, x
    for l in range(DEPTH):
        xc, xl = layer(xc, xl, c, c_ctx, w_ada[l], b_ada[l], norm_w[l], w_in[l], na_rpb[l], four_w[l],
                       ssd_conv_w[l], ssd_conv_b[l], ssd_dt_bias[l], ssd_a_log[l], ssd_d[l], ssd_norm_w[l],
                       wb_na[l], wb_four[l], wb_ssd[l], w_out[l], update_ctx=(l < DEPTH - 1))
    return rmsnorm(xl, final_norm_w)
```

```python
import numpy as np
from contextlib import ExitStack
import concourse.bass as bass
import concourse.mybir as mybir
from concourse.bass_utils import run_bass_kernel_spmd

F32 = mybir.dt.float32
BF16 = mybir.dt.bfloat16
AF = mybir.ActivationFunctionType
ALU = mybir.AluOpType
AX = mybir.AxisListType


class Buf:
    __slots__ = ("name", "w", "r")

    def __init__(self, name):
        self.name = name
        self.w = None
        self.r = []


class Ins:
    __slots__ = ("eng", "fn", "deps", "dma", "count", "marked")

    def __init__(self, eng, fn, deps, dma):
        self.eng = eng
        self.fn = fn
        self.deps = deps
        self.dma = dma
        self.count = 0
        self.marked = False


class Prog:
    ENGS = ("sp", "act", "dve", "pool", "pe")

    def __init__(self, nc):
        self.nc = nc
        self.ins = []
        self.dma_cnt = {}
        self.epoch = 0
        self.epoch_of = []
        self.last_eng = {}
        self.last_dma = {}
        self.pending = {}
        self.keymap = {}

    def fence(self):
        self.keymap = {}
        f = set(self.last_eng.values()) | set(self.last_dma.values())
        for e in self.ENGS:
            self.pending[e] = set(f) | self.pending.get(e, set())

    def new_epoch(self):
        self.epoch += 1

    def add(self, eng, fn, reads=(), writes=(), dma=None):
        if dma is not None:
            if dma not in self.keymap:
                self.keymap[dma] = "k%d" % len(self.keymap)
            dma = self.keymap[dma]
        idx = len(self.ins)
        deps = set()
        for b in reads:
            if b.w is not None:
                deps.add(b.w)
        for b in writes:
            if b.w is not None:
                pw = self.ins[b.w]
                if not (pw.eng == eng and pw.dma is None and dma is None):
                    deps.add(b.w)
            deps.update(b.r)
        if eng in self.pending:
            deps |= self.pending.pop(eng)
        deps.discard(idx)
        if dma is not None:
            self.last_dma[dma] = idx
        else:
            self.last_eng[eng] = idx
        for b in reads:
            b.r.append(idx)
        for b in writes:
            b.w = idx
            b.r = []
        ins = Ins(eng, fn, deps, dma)
        if dma is not None:
            self.dma_cnt[dma] = self.dma_cnt.get(dma, 0) + 16
            ins.count = self.dma_cnt[dma]
        self.ins.append(ins)
        self.epoch_of.append(self.epoch)
        return idx

    def emit(self, stack):
        nc = self.nc
        ins = self.ins
        for i, it in enumerate(ins):
            for d in it.deps:
                dd = ins[d]
                if dd.dma is None:
                    if dd.eng == "pe" and it.eng == "pe":
                        continue
                    dd.marked = True
        cnt = {}
        semkeys = set()
        for i, it in enumerate(ins):
            if it.dma is None and it.marked:
                k = (it.eng, self.epoch_of[i])
                cnt[k] = cnt.get(k, 0) + 1
                it.count = cnt[k]
                semkeys.add(k)
        sems = {}
        for k in sorted(semkeys):
            sems[k] = stack.enter_context(nc.semaphore("s_%s_%d" % k))
        for k in sorted(self.dma_cnt):
            sems[("dma", k)] = stack.enter_context(nc.semaphore("d_" + k))

        def semof(d):
            dd = ins[d]
            if dd.dma is not None:
                return ("dma", dd.dma), dd.count
            return (dd.eng, self.epoch_of[d]), dd.count

        def emit_engine(engname, eng):
            waited = {}
            for i, it in enumerate(ins):
                if it.eng != engname:
                    continue
                for d in sorted(it.deps):
                    dd = ins[d]
                    if dd.dma is None and dd.eng == "pe" and engname == "pe":
                        continue
                    k, v = semof(d)
                    if waited.get(k, 0) >= v:
                        continue
                    eng.wait_ge(sems[k], v)
                    waited[k] = v
                bi = it.fn(eng)
                if it.dma is not None:
                    bi.then_inc(sems[("dma", it.dma)], 16)
                elif it.marked:
                    bi.then_inc(sems[(engname, self.epoch_of[i])], 1)
            if engname == "sp":
                for k in sorted(self.dma_cnt):
                    eng.wait_ge(sems[("dma", k)], self.dma_cnt[k])
                for k in sorted(cnt):
                    eng.wait_ge(sems[k], cnt[k])

        with nc.Block() as block:
            @block.sync
            def _(e):
                emit_engine("sp", e)

            @block.scalar
            def _(e):
                emit_engine("act", e)

            @block.vector
            def _(e):
                emit_engine("dve", e)

            @block.gpsimd
            def _(e):
                emit_engine("pool", e)

            @block.tensor
            def _(e):
                emit_engine("pe", e)


def DMA(P, q, out, in_, r=(), w=(), key=None):
    return P.add(q, lambda e: e.dma_start(out=out, in_=in_), r, w, dma=key)


def MM(P, out, lhsT, rhs, start, stop, r, w):
    return P.add("pe", lambda e: e.matmul(out, lhsT=lhsT, rhs=rhs, start=start, stop=stop), r, w)


def TR(P, out, in_, ident, r, w):
    return P.add("pe", lambda e: e.transpose(out=out, in_=in_, identity=ident), r, w)


def ACT(P, out, in_, func, r, w, **kw):
    return P.add("act", lambda e: e.activation(out=out, in_=in_, func=func, **kw), r, w)


def TT(P, eng, out, in0, in1, op, r, w):
    return P.add(eng, lambda e: e.tensor_tensor(out=out, in0=in0, in1=in1, op=op), r, w)


def TS(P, eng, out, in0, s1, s2, op0, op1, r, w):
    return P.add(eng, lambda e: e.tensor_scalar(out=out, in0=in0, scalar1=s1, scalar2=s2, op0=op0, op1=op1), r, w)


def STT(P, eng, out, in0, scalar, in1, op0, op1, r, w):
    return P.add(eng, lambda e: e.scalar_tensor_tensor(out=out, in0=in0, scalar=scalar, in1=in1, op0=op0, op1=op1), r, w)


def CP(P, eng, out, in_, r, w):
    return P.add(eng, lambda e: e.tensor_copy(out=out, in_=in_), r, w)


def RECIP(P, out, in_, r, w):
    return P.add("dve", lambda e: e.reciprocal(out=out, in_=in_), r, w)


def MEMSET(P, eng, ap, val, w):
    return P.add(eng, lambda e: e.memset(ap, val), (), w)


DM = 2048
NL = 2048
NCX = 256
NT = NL + NCX
NTT = NT // 128
DEPTH = 2
INW = 15392
EPS = 1e-6
C_Q, C_K, C_V, C_ZNA, C_UF, C_ZF, C_XBC, C_ZS, C_DT, C_GNA, C_GF, C_GS = (
    0, 1024, 2048, 3072, 4096, 5120, 6144, 8192, 9216, 9248, 11296, 13344)
R_Q, R_K, R_ZNA, R_UF, R_ZF, R_XBC, R_GNA, R_GF, R_GS = 0, 1024, 2048, 3072, 4096, 5120, 7168, 9216, 11264
PT_ROWS = 13312
TBLK = [(0, 512), (512, 512), (1024, 512), (1536, 512), (2048, 256)]
NBT = 21


class Ctx:
    pass


LATER_PHASES = []


_UID = [0]


def _uid(name):
    _UID[0] += 1
    return "%s_%d" % (name, _UID[0])


def sb(stack, nc, name, shape, dt):
    return stack.enter_context(nc.sbuf_tensor(_uid("sb_" + name), list(shape), dt))


def ps(stack, nc, name, shape, dt=F32):
    return stack.enter_context(nc.psum_tensor(_uid("ps_" + name), list(shape), dt))


class Ring:
    def __init__(self, tiles, prefix):
        self.tiles = tiles
        self.toks = [Buf("%s%d" % (prefix, i)) for i in range(len(tiles))]
        self.i = 0

    def next(self):
        k = self.i % len(self.tiles)
        self.i += 1
        return self.tiles[k], self.toks[k], k


def phase_modh(C, l):
    nc, P = C.nc, C.P
    upd = C.upd[l]
    with ExitStack() as st:
        cc = sb(st, nc, "ma_cc", [128, 16, 2], F32)
        s32 = sb(st, nc, "ma_s32", [128, 16, 2], F32)
        scc = sb(st, nc, "ma_scc", [128, 16, 2], BF16)
        repl = sb(st, nc, "ma_repl", [128, 16, 128], BF16)
        repc = sb(st, nc, "ma_repc", [128, 16, 128], BF16)
        badaT = sb(st, nc, "ma_badaT", [128, 48], F32)
        badag = sb(st, nc, "ma_badag", [128, 2048], F32)
        normwT = sb(st, nc, "ma_normw", [128, 16], F32)
        modT = sb(st, nc, "ma_modT", [128, 32, 2], F32)
        wblk = [sb(st, nc, "ma_w%d" % i, [128, 16, 512], BF16) for i in range(2)]
        pmod = ps(st, nc, "ma_pmod", [128, 32, 2])
        pg = [ps(st, nc, "ma_pg%d" % i, [128, 512]) for i in range(2)]
        b_cc, b_s32, b_scc, b_repl, b_repc = Buf("cc"), Buf("s32"), Buf("scc"), Buf("repl"), Buf("repc")
        b_badaT, b_badag, b_normw, b_modT, b_pmod = Buf("badaT"), Buf("badag"), Buf("normw"), Buf("modT"), Buf("pmod")
        b_w = [Buf("maw0"), Buf("maw1")]
        b_pg = [Buf("mapg0"), Buf("mapg1")]
        xt = [sb(st, nc, "ph_xt%d" % i, [128, DM], F32) for i in range(2)]
        xn = [sb(st, nc, "ph_xn%d" % i, [128, DM], BF16) for i in range(2)]
        junk = sb(st, nc, "ph_junk", [128, DM], BF16)
        ss = [sb(st, nc, "ph_ss%d" % i, [128, 1], F32) for i in range(2)]
        rs = [sb(st, nc, "ph_rs%d" % i, [128, 1], F32) for i in range(2)]
        tp = [ps(st, nc, "ph_tp%d" % i, [128, 16, 128], BF16) for i in range(2)]
        b_xt = [Buf("xt0"), Buf("xt1")]
        b_xn = [Buf("xn0"), Buf("xn1")]
        b_ss = [Buf("ss0"), Buf("ss1")]
        b_rs = [Buf("rs0"), Buf("rs1")]
        b_tp = [Buf("tp0"), Buf("tp1")]
        b_junk = Buf("junk")

        def loadx(t):
            s = t % 2
            src = C.xl_src[l][t * 128:(t + 1) * 128, :] if t < 16 else C.xc_src[l][(t - 16) * 128:(t - 15) * 128, :]
            DMA(P, "sp", xt[s][:], src, w=[b_xt[s]], key="ph_xt%d" % s)

        def htile(t):
            if t + 1 < NTT:
                loadx(t + 1)
            s = t % 2
            ACT(P, junk[:], xt[s][:], AF.Square, [b_xt[s]], [b_junk, b_ss[s]], accum_out=ss[s][:])
            ACT(P, rs[s][:], ss[s][:], AF.Sqrt, [b_ss[s]], [b_rs[s]], scale=1.0 / DM, bias=EPS)
            RECIP(P, rs[s][:], rs[s][:], [b_rs[s]], [b_rs[s]])
            P.add("dve", lambda e, s=s: e.tensor_scalar_mul(out=xn[s][:], in0=xt[s][:], scalar1=rs[s][:, 0:1]),
                  [b_xt[s], b_rs[s]], [b_xn[s]])
            for j in range(16):
                TR(P, tp[s][:, j, :], xn[s][:, j * 128:(j + 1) * 128], C.ident[:], [b_xn[s]], [b_tp[s]])
            out = C.hT[:, :, t * 128:(t + 1) * 128]
            if t % 2 == 0:
                ACT(P, out, tp[s][:], AF.Copy, [b_tp[s]], [])
            else:
                CP(P, "dve", out, tp[s][:], [b_tp[s]], [])

        DMA(P, "sp", cc[:], C.d_cc, w=[b_cc], key="ma_cc")
        DMA(P, "sp", badaT[:], C.d_badaT[l], w=[b_badaT], key="ma_badaT")
        DMA(P, "sp", badag[:], C.d_bada[l][:, 4096:6144].partition_broadcast(128), w=[b_badag], key="ma_badag")
        DMA(P, "sp", normwT[:], C.d_normwT[l], w=[b_normw], key="ma_normw")
        loadx(0)
        ACT(P, s32[:], cc[:], AF.Silu, [b_cc], [b_s32])
        CP(P, "dve", scc[:], s32[:], [b_s32], [b_scc])
        CP(P, "dve", repl[:], s32[:, :, 0:1].to_broadcast([128, 16, 128]), [b_s32], [b_repl])
        if upd:
            CP(P, "dve", repc[:], s32[:, :, 1:2].to_broadcast([128, 16, 128]), [b_s32], [b_repc])
        wv = C.d_wada[l].rearrange("(j p) n -> p j n", p=128)

        def load(b):
            s = b % 2
            DMA(P, "pool", wblk[s][:], wv[:, :, b * 512:(b + 1) * 512], w=[b_w[s]], key="ma_w%d" % s)

        load(0)
        k = 0
        tnext = 0
        for b in range(12):
            if b + 1 < 12:
                load(b + 1)
            for _ in range(2 if b % 2 == 0 else 1):
                htile(tnext)
                tnext += 1
            s = b % 2
            if b < 8:
                for nt in range(4):
                    n = b * 4 + nt
                    for j in range(16):
                        MM(P, pmod[:, n, :], wblk[s][:, j, nt * 128:(nt + 1) * 128], scc[:, j, :], j == 0, j == 15,
                           [b_w[s], b_scc], [b_pmod])
            else:
                gb = b - 8
                for which in ([0, 1] if upd else [0]):
                    rep = repl if which == 0 else repc
                    brep = b_repl if which == 0 else b_repc
                    pgt, bpg = pg[k % 2], b_pg[k % 2]
                    k += 1
                    for j in range(16):
                        MM(P, pgt[:], rep[:, j, :], wblk[s][:, j, :], j == 0, j == 15, [b_w[s], brep], [bpg])
                    dst = (C.glb if which == 0 else C.gcb)[:, gb * 512:(gb + 1) * 512]
                    TT(P, "dve", dst, pgt[:], badag[:, gb * 512:(gb + 1) * 512], ALU.add, [bpg, b_badag], [C.b_mod])
        assert tnext == NTT
        TT(P, "dve", modT[:], pmod[:], badaT[:, 0:32].unsqueeze(2).to_broadcast([128, 32, 2]), ALU.add,
           [b_pmod, b_badaT], [b_modT])
        for which in range(2):
            STT(P, "dve", C.w1[:, which, :], modT[:, 16:32, which], 1.0, normwT[:], ALU.add, ALU.mult,
                [b_modT, b_normw], [C.b_mod])
            CP(P, "dve", C.sh[:, which, :], modT[:, 0:16, which], [b_modT], [C.b_mod])
        P.fence()
        for j in range(16):
            for wh, (c0, c1) in enumerate([(0, NL), (NL, NT)]):
                v = C.hT[:, j, c0:c1]
                if j % 2 == 0:
                    ACT(P, v, v, AF.Identity, [C.b_mod], [], scale=C.w1[:, wh, j:j + 1], bias=C.sh[:, wh, j:j + 1])
                else:
                    STT(P, "dve", v, v, C.w1[:, wh, j:j + 1], C.sh[:, wh, j:j + 1].to_broadcast([128, c1 - c0]),
                        ALU.mult, ALU.add, [C.b_mod], [])
    P.fence()


def phase_gemm(C, l):
    nc, P = C.nc, C.P
    with ExitStack() as st:
        NWB = 3
        wblk = [sb(st, nc, "pc_w%d" % i, [128, 16, 512], BF16) for i in range(NWB)]
        ost = [sb(st, nc, "pc_o%d" % i, [128, NT], BF16) for i in range(3)]
        tst = [sb(st, nc, "pc_t%d" % i, [128, NTT, 512], BF16) for i in range(2)]
        wdt = sb(st, nc, "pc_wdt", [128, 16, 32], BF16)
        dtst = sb(st, nc, "pc_dtst", [128, NTT, 32], F32)
        dtb = sb(st, nc, "pc_dtb", [128, 32], F32)
        ring = Ring([ps(st, nc, "pc_p%d" % i, [128, 512]) for i in range(8)], "pcp")
        b_w = [Buf("pcw%d" % i) for i in range(NWB)]
        b_o = [Buf("pco%d" % i) for i in range(3)]
        b_t = [Buf("pct%d" % i) for i in range(2)]
        b_wdt, b_dtst, b_dtb = Buf("wdt"), Buf("dtst"), Buf("dtb")
        wv = C.d_win[l].rearrange("(j p) n -> p j n", p=128)

        fm = [(C_Q, R_Q, 1024, "q"), (C_K, R_K, 1024, "copy"), (C_ZNA, R_ZNA, 1024, "silu"), (C_UF, R_UF, 1024, "copy"),
              (C_ZF, R_ZF, 1024, "silu"), (C_XBC, R_XBC, 2048, "copy"), (C_GNA, R_GNA, 6144, "sigmoid")]
        blocks = []
        for (c0, r0, n, fn) in fm:
            for b in range(n // 512):
                blocks.append(("fm", c0 + b * 512, r0 + b * 512, fn))
        for b in range(2):
            blocks.append(("tm", C_V + b * 512, (C.TM_v, b), "copy"))
        for b in range(2):
            blocks.append(("tm", C_ZS + b * 512, (C.TM_zs, b), "silu"))

        def load(i):
            s = i % NWB
            c0 = blocks[i][1]
            DMA(P, "pool", wblk[s][:], wv[:, :, c0:c0 + 512], w=[b_w[s]], key="pc_w%d" % s)

        def evac(fn, out, in_, r, w):
            if fn == "copy":
                CP(P, "dve", out, in_, r, w)
            elif fn == "q":
                P.add("dve", lambda e: e.tensor_scalar_mul(out=out, in0=in_, scalar1=float(128 ** -0.5)), r, w)
            elif fn == "silu":
                ACT(P, out, in_, AF.Silu, r, w)
            elif fn == "sigmoid":
                ACT(P, out, in_, AF.Sigmoid, r, w)

        DMA(P, "sp", dtb[:], C.d_dtbias[l].partition_broadcast(128), w=[b_dtb], key="pc_dtb")
        DMA(P, "pool", wdt[:], wv[:, :, C_DT:C_DT + 32], w=[b_wdt], key="pc_wdt")
        load(0)
        load(1)
        oi = 0
        ti = 0
        for i, (kind, c0, dst, fn) in enumerate(blocks):
            if i + 2 < len(blocks):
                load(i + 2)
            s = i % NWB
            if kind == "fm":
                need_ctx = C.upd[l] or dst in (R_K, R_K + 512, R_XBC, R_XBC + 512, R_XBC + 1024, R_XBC + 1536)
                tbl = TBLK if need_ctx else TBLK[:4]
                ncol = NT if need_ctx else NL
                for nt in range(4):
                    o, bo = ost[oi % 3], b_o[oi % 3]
                    for (t0, tsz) in tbl:
                        pk, bp, _ = ring.next()
                        for j in range(16):
                            MM(P, pk[:, 0:tsz], wblk[s][:, j, nt * 128:(nt + 1) * 128], C.hT[:, j, t0:t0 + tsz],
                               j == 0, j == 15, [b_w[s]], [bp])
                        evac(fn, o[:, t0:t0 + tsz], pk[:, 0:tsz], [bp], [bo])
                    DMA(P, "sp", C.PT[dst + nt * 128:dst + (nt + 1) * 128, 0:ncol], o[:, 0:ncol], r=[bo], key="pc_o%d" % (oi % 3))
                    oi += 1
            else:
                tm, b = dst
                o, bo = tst[ti % 2], b_t[ti % 2]
                ntm = NTT if (C.upd[l] or fn == "copy") else 16
                for t in range(ntm):
                    pk, bp, _ = ring.next()
                    for j in range(16):
                        MM(P, pk[:], C.hT[:, j, t * 128:(t + 1) * 128], wblk[s][:, j, :], j == 0, j == 15, [b_w[s]], [bp])
                    evac(fn, o[:, t, :], pk[:], [bp], [bo])
                DMA(P, "sp", tm.rearrange("(t p) n -> p t n", p=128)[:, 0:ntm, b * 512:(b + 1) * 512], o[:, 0:ntm, :], r=[bo],
                    key="pc_t%d" % (ti % 2))
                ti += 1
        for t in range(NTT):
            pk, bp, _ = ring.next()
            for j in range(16):
                MM(P, pk[:, 0:32], C.hT[:, j, t * 128:(t + 1) * 128], wdt[:, j, :], j == 0, j == 15, [b_wdt], [bp])
            TT(P, "dve", dtst[:, t, :], pk[:, 0:32], dtb[:], ALU.add, [bp, b_dtb], [b_dtst])
        ACT(P, dtst[:], dtst[:], AF.Exp, [b_dtst], [b_dtst])
        ACT(P, dtst[:], dtst[:], AF.Ln, [b_dtst], [b_dtst], bias=1.0)
        DMA(P, "sp", C.TM_dt.rearrange("(t p) n -> p t n", p=128), dtst[:], r=[b_dtst], key="pc_dtst")
    P.fence()


def phase_attn(C, l):
    nc, P = C.nc, C.P
    upd = C.upd[l]
    npairs = 18 if upd else 16
    NS = 3
    with ExitStack() as st:
        QT = [sb(st, nc, "pd_q%d" % i, [128, NT], BF16) for i in range(2)]
        KT = [sb(st, nc, "pd_k%d" % i, [128, NT], BF16) for i in range(2)]
        ZT = [sb(st, nc, "pd_z%d" % i, [128, NT], BF16) for i in range(2)]
        V = [sb(st, nc, "pd_v%d" % i, [128, NTT, 128], BF16) for i in range(2)]
        Et = [sb(st, nc, "pd_e%d" % i, [128, NBT, 128], F32) for i in range(2)]
        gat = [sb(st, nc, "pd_g%d" % i, [128, NT], BF16) for i in range(2)]
        ones = sb(st, nc, "pd_ones", [128, 128], BF16)
        Pw = [sb(st, nc, "pd_pw%d" % i, [128, 5, 128], F32) for i in range(NS)]
        Pb = [sb(st, nc, "pd_pb%d" % i, [128, 7, 128], BF16) for i in range(NS)]
        rden = [sb(st, nc, "pd_rd%d" % i, [128, 128], F32) for i in range(2)]
        otmp = [sb(st, nc, "pd_ot%d" % i, [128, 128], F32) for i in range(2)]
        Sps = [ps(st, nc, "pd_s%d" % i, [128, 1024]) for i in range(NS)]
        OD = [ps(st, nc, "pd_od%d" % i, [128, 2, 128]) for i in range(2)]
        mk = lambda n, k: [Buf("%s%d" % (n, i)) for i in range(k)]
        b_qkv, b_E, b_gat = mk("qkv", 2), mk("E", 2), mk("gat", 2)
        b_Pw, b_Pbw, b_Pbc, b_S = mk("Pw", NS), mk("Pbw", NS), mk("Pbc", NS), mk("S", NS)
        b_rd, b_ot, b_OD = mk("rd", 2), mk("ot", 2), mk("OD", 2)
        b_ones = Buf("ones")
        MEMSET(P, "dve", ones[:], 1.0, [b_ones])
        vview = C.TM_v.rearrange("(t p) n -> p t n", p=128)

        def load(h):
            s = h % 2
            DMA(P, "sp", QT[s][:], C.PT[R_Q + h * 128:R_Q + (h + 1) * 128, :], w=[b_qkv[s]], key="pd_q%d" % s)
            DMA(P, "sp", KT[s][:], C.PT[R_K + h * 128:R_K + (h + 1) * 128, :], w=[b_qkv[s]], key="pd_k%d" % s)
            DMA(P, "sp", ZT[s][:], C.PT[R_ZNA + h * 128:R_ZNA + (h + 1) * 128, :], w=[b_qkv[s]], key="pd_z%d" % s)
            DMA(P, "sp", V[s][:], vview[:, :, h * 128:(h + 1) * 128], w=[b_qkv[s]], key="pd_v%d" % s)
            DMA(P, "sp", Et[s][:], C.d_rpb[l, h], w=[b_E[s]], key="pd_e%d" % s)

        def expE(h):
            s = h % 2
            ACT(P, Et[s][:], Et[s][:], AF.Exp, [b_E[s]], [b_E[s]])

        def geom(i):
            if i >= 16:
                alist, e0 = [], 0
            elif 2 <= i <= 13:
                alist, e0 = list(range(i - 2, i + 3)), 0
            elif i < 2:
                alist, e0 = [0, 1, 2, 3], 5 + 4 * i
            else:
                alist, e0 = [12, 13, 14, 15], 13 + 4 * (i - 14)
            return alist, e0

        items = [(h, i) for h in range(8) for i in range(npairs)]

        def stage_s(n):
            h, i = items[n]
            hs, p_ = h % 2, n % NS
            alist, e0 = geom(i)
            kt = alist + [16, 17]
            q0 = i * 128
            for si, a in enumerate(kt):
                MM(P, Sps[p_][:, si * 128:(si + 1) * 128], KT[hs][:, a * 128:(a + 1) * 128], QT[hs][:, q0:q0 + 128], True, True,
                   [b_qkv[hs]], [b_S[p_]])

        def stage_p(n):
            h, i = items[n]
            hs, p_ = h % 2, n % NS
            alist, e0 = geom(i)
            nw = len(alist)
            nk = nw + 2
            Sp = Sps[p_]
            if nw:
                ACT(P, Pw[p_][:, 0:nw, :], Sp[:, 0:nw * 128].rearrange("p (a q) -> p a q", q=128), AF.Exp,
                    [b_S[p_]], [b_Pw[p_]])
                TT(P, "pool", Pb[p_][:, 0:nw, :], Pw[p_][:, 0:nw, :], Et[hs][:, e0:e0 + nw, :], ALU.mult,
                   [b_Pw[p_], b_E[hs]], [b_Pbw[p_]])
            ACT(P, Pb[p_][:, nw:nk, :], Sp[:, nw * 128:nk * 128].rearrange("p (a q) -> p a q", q=128), AF.Exp,
                [b_S[p_]], [b_Pbc[p_]])

        def stage_o(n):
            h, i = items[n]
            hs, p_, o_ = h % 2, n % NS, n % 2
            alist, e0 = geom(i)
            kt = alist + [16, 17]
            nk = len(kt)
            q0 = i * 128
            for si, a in enumerate(kt):
                MM(P, OD[o_][:, 0, :], V[hs][:, a, :], Pb[p_][:, si, :], si == 0, si == nk - 1,
                   [b_qkv[hs], b_Pbw[p_], b_Pbc[p_]], [b_OD[o_]])
            for si in range(nk):
                MM(P, OD[o_][:, 1, :], ones[:], Pb[p_][:, si, :], si == 0, si == nk - 1,
                   [b_ones, b_Pbw[p_], b_Pbc[p_]], [b_OD[o_]])
            RECIP(P, rden[o_][:], OD[o_][:, 1, :], [b_OD[o_]], [b_rd[o_]])
            TT(P, "dve", otmp[o_][:], OD[o_][:, 0, :], rden[o_][:], ALU.mult, [b_OD[o_], b_rd[o_]], [b_ot[o_]])
            TT(P, "dve", gat[hs][:, q0:q0 + 128], otmp[o_][:], ZT[hs][:, q0:q0 + 128], ALU.mult,
               [b_ot[o_], b_qkv[hs]], [b_gat[hs]])
            if i == npairs - 1:
                DMA(P, "sp", C.GaT[h * 128:(h + 1) * 128, 0:npairs * 128], gat[hs][:, 0:npairs * 128], r=[b_gat[hs]],
                    key="pd_g%d" % hs)

        load(0)
        expE(0)
        load(1)
        N = len(items)
        stage_s(0)
        if N > 1:
            stage_s(1)
        stage_p(0)
        for n in range(N):
            h, i = items[n]
            if i == 0 and h >= 1 and h + 1 < 8:
                load(h + 1)
            if i == npairs // 2 and h + 1 < 8:
                expE(h + 1)
            if n + 2 < N:
                stage_s(n + 2)
            if n + 1 < N:
                stage_p(n + 1)
            stage_o(n)
    P.fence()


def phase_four(C, l):
    nc, P = C.nc, C.P
    upd = C.upd[l]
    ntt = NTT if upd else 16
    with ExitStack() as st:
        UCS = sb(st, nc, "pe_ucs", [128, NTT, 4, 512], BF16)
        b_ucs = [Buf("ucsA"), Buf("ucsD")]
        with ExitStack() as s1:
            cs1 = sb(s1, nc, "pe_cs1", [128, 2, 512], BF16)
            ufT = [sb(s1, nc, "pe_uf%d" % i, [128, 2, NT], BF16) for i in range(2)]
            ring = Ring([ps(s1, nc, "pe_p1_%d" % i, [128, 512]) for i in range(4)], "pe1")
            b_cs1 = Buf("cs1")
            b_uf = [Buf("uf0"), Buf("uf1")]
            DMA(P, "sp", cs1[:], C.d_cs1.rearrange("(c p) n -> p c n", p=128), w=[b_cs1], key="pe_cs1")

            def loadu(g):
                DMA(P, "sp", ufT[g % 2][:], C.PT[R_UF + g * 256:R_UF + (g + 1) * 256, :].rearrange("(c p) t -> p c t", p=128),
                    w=[b_uf[g % 2]], key="pe_uf%d" % (g % 2))

            loadu(0)
            k = 0
            for g in range(4):
                if g + 1 < 4:
                    loadu(g + 1)
                for t in range(ntt):
                    pk, bp, _ = ring.next()
                    for c in range(2):
                        MM(P, pk[:], ufT[g % 2][:, c, t * 128:(t + 1) * 128], cs1[:, c, :], c == 0, c == 1,
                           [b_uf[g % 2], b_cs1], [bp])
                    if k % 2 == 0:
                        ACT(P, UCS[:, t, g, :], pk[:], AF.Copy, [bp], [b_ucs[0]])
                    else:
                        CP(P, "dve", UCS[:, t, g, :], pk[:], [bp], [b_ucs[1]])
                    k += 1
        P.fence()
        with ExitStack() as s2:
            DC = [sb(s2, nc, "pe_dc%d" % i, [128, 16, 256], BF16) for i in range(2)]
            DS = [sb(s2, nc, "pe_ds%d" % i, [128, 16, 256], BF16) for i in range(2)]
            cs2 = sb(s2, nc, "pe_cs2", [128, 2, 512], BF16)
            fw = sb(s2, nc, "pe_fw", [128, 8, 1024], BF16)
            FT = [sb(s2, nc, "pe_ft%d" % i, [128, 8, 256], BF16) for i in range(2)]
            Zf = [sb(s2, nc, "pe_zf%d" % i, [128, 8, 256], BF16) for i in range(2)]
            gst = [sb(s2, nc, "pe_gs%d" % i, [128, 8, 256], BF16) for i in range(2)]
            ring = Ring([ps(s2, nc, "pe_p2_%d" % i, [128, 256]) for i in range(6)], "pe2")
            b_tab = [Buf("tab0"), Buf("tab1")]
            b_cs2, b_fw = Buf("cs2"), Buf("fw")
            b_ft = [Buf("ft0"), Buf("ft1")]
            b_zf = [Buf("zf0"), Buf("zf1")]
            b_gs = [Buf("gs0"), Buf("gs1")]
            DMA(P, "sp", cs2[:], C.d_cs2.rearrange("(c p) n -> p c n", p=128), w=[b_cs2], key="pe_cs2")
            DMA(P, "pool", fw[:], C.d_fourw[l].rearrange("(c p) n -> p c n", p=128), w=[b_fw], key="pe_fw")
            fin = FinWork(C, l)
            fin.alloc(s2)
            fin_next, fin_pending = 0, []
            blocks = [(lb * 256, "lat") for lb in range(8)] + ([(NL, "ctx")] if upd else [])
            dcv = C.d_dftc.rearrange("(lt p) n -> p lt n", p=128)
            dsv = C.d_dfts.rearrange("(lt p) n -> p lt n", p=128)

            def loadb(bi):
                t0, kind = blocks[bi]
                s = bi % 2
                if kind == "lat":
                    DMA(P, "sp", DC[s][:], dcv[:, :, t0:t0 + 256], w=[b_tab[s]], key="pe_dc%d" % s)
                    DMA(P, "sp", DS[s][:], dsv[:, :, t0:t0 + 256], w=[b_tab[s]], key="pe_ds%d" % s)
                DMA(P, "sp", Zf[s][:], C.PT[R_ZF:R_ZF + 1024, t0:t0 + 256].rearrange("(m p) t -> p m t", p=128),
                    w=[b_zf[s]], key="pe_zf%d" % s)

            loadb(0)
            for bi, (t0, kind) in enumerate(blocks):
                if bi + 1 < len(blocks):
                    loadb(bi + 1)
                s = bi % 2
                scale = float((2048.0 * 256.0) ** -0.5) if kind == "lat" else float(1.0 / 256.0)
                for g in range(4):
                    for ct in range(2):
                        pk, bp, _ = ring.next()
                        if kind == "lat":
                            for lt in range(16):
                                MM(P, pk[:], UCS[:, lt, g, ct * 128:(ct + 1) * 128], DC[s][:, lt, :], lt == 0, False,
                                   [b_tab[s]], [bp])
                                MM(P, pk[:], UCS[:, lt, g, 256 + ct * 128:256 + (ct + 1) * 128], DS[s][:, lt, :], False, lt == 15,
                                   [b_tab[s]], [bp])
                        else:
                            for lt in range(2):
                                MM(P, pk[:], UCS[:, 16 + lt, g, ct * 128:(ct + 1) * 128], cs2[:, lt, 0:256], lt == 0, False,
                                   [b_cs2], [bp])
                                MM(P, pk[:], UCS[:, 16 + lt, g, 256 + ct * 128:256 + (ct + 1) * 128], cs2[:, lt, 256:512], False, lt == 1,
                                   [b_cs2], [bp])
                        ACT(P, FT[s][:, g * 2 + ct, :], pk[:], AF.Copy, [bp], [b_ft[s]], scale=scale)
                for m in range(8):
                    pk, bp, _ = ring.next()
                    for cch in range(8):
                        MM(P, pk[:], fw[:, cch, m * 128:(m + 1) * 128], FT[s][:, cch, :], cch == 0, cch == 7,
                           [b_fw, b_ft[s]], [bp])
                    TT(P, "dve", gst[s][:, m, :], pk[:], Zf[s][:, m, :], ALU.mult, [bp, b_zf[s]], [b_gs[s]])
                DMA(P, "sp", C.GfT[:, t0:t0 + 256].rearrange("(m p) t -> p m t", p=128), gst[s][:], r=[b_gs[s]],
                    key="pe_gs%d" % s)
                for t in fin_pending:
                    fin.finish(t)
                fin_pending = []
                for _ in range(2):
                    if fin_next < fin.ntt:
                        fin.chain(fin_next)
                        fin_pending.append(fin_next)
                        fin_next += 1
            while fin_next < fin.ntt:
                fin.chain(fin_next)
                fin_pending.append(fin_next)
                fin_next += 1
            for t in fin_pending:
                fin.finish(t)
    P.fence()


def phase_ssd_prep(C, l):
    nc, P = C.nc, C.P
    with ExitStack() as st:
        cw = sb(st, nc, "f1_cw", [128, 16, 7], F32)
        cb = sb(st, nc, "f1_cb", [128, 16], F32)
        cosT = sb(st, nc, "f1_cos", [128, NL], F32)
        sinT = sb(st, nc, "f1_sin", [128, NL], F32)
        rot = sb(st, nc, "f1_rot", [128, 128], BF16)
        xin = [sb(st, nc, "f1_xin%d" % i, [128, NT], BF16) for i in range(2)]
        dg = [sb(st, nc, "f1_dg%d" % i, [128, 7, 128], BF16) for i in range(2)]
        xo = [sb(st, nc, "f1_xo%d" % i, [128, NT], BF16) for i in range(2)]
        xr = [sb(st, nc, "f1_xr%d" % i, [128, NT], BF16) for i in range(2)]
        t1 = [sb(st, nc, "f1_t1%d" % i, [128, 512], F32) for i in range(2)]
        t2 = [sb(st, nc, "f1_t2%d" % i, [128, 512], F32) for i in range(2)]
        tst = [sb(st, nc, "f1_ts%d" % i, [128, NTT, 128], BF16) for i in range(2)]
        rcv = Ring([ps(st, nc, "f1_pc%d" % i, [128, 512]) for i in range(3)], "f1pc")
        rrt = Ring([ps(st, nc, "f1_pr%d" % i, [128, 512]) for i in range(2)], "f1pr")
        rtr = Ring([ps(st, nc, "f1_pt%d" % i, [128, 8, 128], BF16) for i in range(2)], "f1pt")
        b_cw, b_cb, b_cs, b_rot = Buf("cw"), Buf("cb"), Buf("cs"), Buf("rot")
        b_xin = [Buf("xin0"), Buf("xin1")]
        b_dg = [Buf("dg0"), Buf("dg1")]
        b_xo = [Buf("xo0"), Buf("xo1")]
        b_xr = [Buf("xr0"), Buf("xr1")]
        b_t1 = [Buf("t10"), Buf("t11")]
        b_t2 = [Buf("t20"), Buf("t21")]
        b_ts = [Buf("ts0"), Buf("ts1")]
        DMA(P, "sp", cw[:], C.d_convw[l], w=[b_cw], key="f1_cw")
        DMA(P, "sp", cb[:], C.d_convb[l], w=[b_cb], key="f1_cb")
        DMA(P, "sp", cosT[:], C.d_ropec, w=[b_cs], key="f1_cos")
        DMA(P, "sp", sinT[:], C.d_ropes, w=[b_cs], key="f1_sin")
        DMA(P, "sp", rot[:], C.d_rot, w=[b_rot], key="f1_rot")

        def load(c):
            DMA(P, "sp", xin[c % 2][:], C.PT[R_XBC + c * 128:R_XBC + (c + 1) * 128, :], w=[b_xin[c % 2]], key="f1_xin%d" % (c % 2))

        segs = [(0, NL), (NL, NT)]
        cnt = {"tsi": 0, "ki": 0}
        srcs = {}

        def stage1(c):
            if c + 1 < 16:
                load(c + 1)
            s = c % 2
            for j in range(7):
                P.add("dve", lambda e, s=s, j=j, c=c: e.tensor_scalar_mul(out=dg[s][:, j, :], in0=C.ident[:], scalar1=cw[:, c, j:j + 1]),
                      [b_cw], [b_dg[s]])
            for (t0, tsz) in TBLK:
                lo_seg, hi_seg = segs[0] if t0 < NL else segs[1]
                pk, bp, _ = rcv.next()
                MM(P, pk[:, 0:tsz], dg[s][:, 3, :], xin[s][:, t0:t0 + tsz], True, False, [b_dg[s], b_xin[s]], [bp])
                for j in [0, 1, 2, 4, 5, 6]:
                    sh_ = j - 3
                    lo = max(t0 + sh_, lo_seg)
                    hi = min(t0 + tsz + sh_, hi_seg)
                    MM(P, pk[:, lo - sh_ - t0:hi - sh_ - t0], dg[s][:, j, :], xin[s][:, lo:hi], False, j == 6,
                       [b_dg[s], b_xin[s]], [bp])
                ACT(P, xo[s][:, t0:t0 + tsz], pk[:, 0:tsz], AF.Silu, [bp, b_cb], [b_xo[s]], bias=cb[:, c:c + 1])
            srcs[c] = (xo[s], b_xo[s])
            if c >= 8:
                for (t0, tsz) in TBLK[:4]:
                    pk, bp, _ = rrt.next()
                    k_ = cnt["ki"] % 2
                    cnt["ki"] += 1
                    MM(P, pk[:], rot[:], xo[s][:, t0:t0 + 512], True, True, [b_rot, b_xo[s]], [bp])
                    TT(P, "dve", t1[k_][:], xo[s][:, t0:t0 + 512], cosT[:, t0:t0 + 512], ALU.mult, [b_xo[s], b_cs], [b_t1[k_]])
                    TT(P, "dve", t2[k_][:], pk[:], sinT[:, t0:t0 + 512], ALU.mult, [bp, b_cs], [b_t2[k_]])
                    TT(P, "pool", xr[s][:, t0:t0 + 512], t1[k_][:], t2[k_][:], ALU.add, [b_t1[k_], b_t2[k_]], [b_xr[s]])
                CP(P, "pool", xr[s][:, NL:NT], xo[s][:, NL:NT], [b_xo[s]], [b_xr[s]])
                srcs[c] = (xr[s], b_xr[s])
                g = (c - 8) % 4
                dst = C.BT if c < 12 else C.CT
                DMA(P, "sp", dst[g * 128:(g + 1) * 128, :], xr[s][:], r=[b_xr[s]], key="f1_xr%d" % s)

        def stage2(c):
            if c >= 12:
                return
            src, bsrc = srcs[c]
            ts_ = cnt["tsi"] % 2
            cnt["tsi"] += 1
            for tg in range(3):
                n = min(8, NTT - tg * 8)
                pt, bpt, _ = rtr.next()
                for q in range(n):
                    t = tg * 8 + q
                    TR(P, pt[:, q, :], src[:, t * 128:(t + 1) * 128], C.ident[:], [bsrc], [bpt])
                CP(P, "dve", tst[ts_][:, tg * 8:tg * 8 + n, :], pt[:, 0:n, :], [bpt], [b_ts[ts_]])
            if c < 8:
                dv = C.XS.rearrange("(t p) n -> p t n", p=128)[:, :, c * 128:(c + 1) * 128]
            else:
                dv = C.Btm.rearrange("(t p) n -> p t n", p=128)[:, :, (c - 8) * 128:(c - 7) * 128]
            DMA(P, "sp", dv, tst[ts_][:], r=[b_ts[ts_]], key="f1_ts%d" % ts_)

        load(0)
        stage1(0)
        for c in range(16):
            if c + 1 < 16:
                stage1(c + 1)
            stage2(c)
    P.fence()


def phase_ssd_scan(C, l):
    nc, P = C.nc, C.P
    upd = C.upd[l]
    NLD, NPR, NU = 4, 3, 3
    with ExitStack() as st:
        masks = sb(st, nc, "f2_masks", [128, 4, 128], F32)
        alog = sb(st, nc, "f2_alog", [128, 32], F32)
        aneg = sb(st, nc, "f2_aneg", [128, 32], F32)
        ones32 = sb(st, nc, "f2_ones", [128, 128], F32)
        S32 = [[sb(st, nc, "f2_S%d%d" % (d, g), [128, 256], F32) for g in range(4)] for d in range(2)]
        Sbf = [[sb(st, nc, "f2_Sb%d%d" % (d, g), [128, 256], BF16) for g in range(4)] for d in range(2)]
        dtc = [sb(st, nc, "f2_dt%d" % i, [128, 32], F32) for i in range(NLD)]
        xsc = [sb(st, nc, "f2_xs%d" % i, [128, 16, 64], BF16) for i in range(NLD)]
        bc = [sb(st, nc, "f2_b%d" % i, [128, 512], BF16) for i in range(NLD)]
        btc = [sb(st, nc, "f2_bt%d" % i, [128, 4, 128], BF16) for i in range(NLD)]
        ctc = [sb(st, nc, "f2_ct%d" % i, [128, 4, 128], BF16) for i in range(NLD)]
        A = [sb(st, nc, "f2_A%d" % i, [128, 16], F32) for i in range(NPR)]
        ex = [sb(st, nc, "f2_ex%d" % i, [128, 3, 16], F32) for i in range(NPR)]
        xdt = [sb(st, nc, "f2_xdt%d" % i, [128, 16, 64], BF16) for i in range(NPR)]
        xdtd = [sb(st, nc, "f2_xdd%d" % i, [128, 16, 64], BF16) for i in range(NPR)]
        rhsA = [sb(st, nc, "f2_rA%d" % i, [128, 16, 128], F32) for i in range(NPR)]
        expD = [sb(st, nc, "f2_eD%d" % i, [128, 4, 128], BF16) for i in range(NU)]
        Gm = [sb(st, nc, "f2_Gm%d" % i, [128, 128], BF16) for i in range(NU)]
        MT = [sb(st, nc, "f2_MT%d" % i, [128, 4, 128], BF16) for i in range(NU)]
        ytmp = [sb(st, nc, "f2_yt%d" % i, [128, 4, 64], F32) for i in range(2)]
        ysum = [sb(st, nc, "f2_ys%d" % i, [128, 1024], F32) for i in range(2)]
        pD = [ps(st, nc, "f2_pD%d" % i, [128, 512]) for i in range(2)]
        pG = [ps(st, nc, "f2_pG%d" % i, [128, 128]) for i in range(2)]
        pY = [ps(st, nc, "f2_pY%d" % i, [128, 512]) for i in range(2)]
        pS = ps(st, nc, "f2_pS", [128, 256])
        psm = ps(st, nc, "f2_psm", [128, 3, 16])
        mk = lambda n, k: [Buf("%s%d" % (n, i)) for i in range(k)]
        b_pD, b_pG, b_pY = mk("pD", 2), mk("pG", 2), mk("pY", 2)
        b_pS, b_psm = Buf("pS"), Buf("psm")
        b_masks, b_a, b_ones = Buf("masks"), Buf("aneg"), Buf("ones32")
        b_S32 = [[Buf("S32%d%d" % (d, g)) for g in range(4)] for d in range(2)]
        b_Sbf = [[Buf("Sbf%d%d" % (d, g)) for g in range(4)] for d in range(2)]
        b_dt, b_xs, b_b, b_bt, b_ct = mk("dt", NLD), mk("xs", NLD), mk("b", NLD), mk("bt", NLD), mk("ct", NLD)
        b_A, b_ex, b_xdt, b_xdd, b_rA = mk("A", NPR), mk("ex", NPR), mk("xdt", NPR), mk("xdd", NPR), mk("rA", NPR)
        b_eD, b_Gm, b_MT = mk("eD", NU), mk("Gm", NU), mk("MT", NU)
        b_yt, b_ys = mk("yt", 2), mk("ys", 2)

        DMA(P, "sp", masks[:], C.d_masks, w=[b_masks], key="f2_masks")
        DMA(P, "sp", alog[:], C.d_alog[l].partition_broadcast(128), w=[b_a], key="f2_alog")
        ACT(P, aneg[:], alog[:], AF.Exp, [b_a], [b_a])
        P.add("dve", lambda e: e.tensor_scalar_mul(out=aneg[:], in0=aneg[:], scalar1=-1.0), [b_a], [b_a])
        MEMSET(P, "dve", ones32[:], 1.0, [b_ones])
        for d in range(2):
            for g in range(4):
                MEMSET(P, "dve", S32[d][g][:], 0.0, [b_S32[d][g]])
                MEMSET(P, "pool", Sbf[d][g][:], 0.0, [b_Sbf[d][g]])
        order = [[16, 17] + list(range(16)), [17, 16] + list(range(15, -1, -1))]
        steps = []
        for i in range(NTT):
            steps.append((0, order[0][i]))
            steps.append((1, order[1][i]))
        nsteps = len(steps)
        xsv = C.XS.rearrange("t (h p) -> t h p", p=64)

        def need_y(k):
            return (steps[k][1] < 16) or upd

        def load(k):
            d, ch = steps[k]
            s = k % NLD
            r0 = ch * 128
            DMA(P, "sp", dtc[s][:], C.TM_dt[r0:r0 + 128, :], w=[b_dt[s]], key="f2_dt%d" % s)
            DMA(P, "sp", xsc[s][:], xsv[r0:r0 + 128], w=[b_xs[s]], key="f2_xs%d" % s)
            DMA(P, "sp", bc[s][:], C.Btm[r0:r0 + 128, :], w=[b_b[s]], key="f2_b%d" % s)
            DMA(P, "sp", btc[s][:], C.BT[:, r0:r0 + 128].rearrange("(g n) s -> n g s", n=128), w=[b_bt[s]], key="f2_bt%d" % s)
            DMA(P, "sp", ctc[s][:], C.CT[:, r0:r0 + 128].rearrange("(g n) s -> n g s", n=128), w=[b_ct[s]], key="f2_ct%d" % s)

        def mset(d):
            return (0, 1, 1) if d == 0 else (2, 3, 3)

        def stepprep(k):
            d, ch = steps[k]
            sl, s = k % NLD, k % NPR
            m_inc, m_rest, m_D = mset(d)
            dsl = slice(d * 16, (d + 1) * 16)
            TT(P, "dve", A[s][:], dtc[sl][:, dsl], aneg[:, dsl], ALU.mult, [b_dt[sl], b_a], [b_A[s]])
            MM(P, psm[:, 0, :], masks[:, m_inc, :], A[s][:], True, True, [b_masks, b_A[s]], [b_psm])
            MM(P, psm[:, 1, :], masks[:, m_rest, :], A[s][:], True, True, [b_masks, b_A[s]], [b_psm])
            MM(P, psm[:, 2, :], ones32[:], A[s][:], True, True, [b_ones, b_A[s]], [b_psm])
            ACT(P, ex[s][:], psm[:], AF.Exp, [b_psm], [b_ex[s]])
            TT(P, "dve", xdt[s][:], xsc[sl][:], dtc[sl][:, dsl].unsqueeze(2).to_broadcast([128, 16, 64]), ALU.mult,
               [b_xs[sl], b_dt[sl]], [b_xdt[s]])
            TT(P, "pool", xdtd[s][:], xdt[s][:], ex[s][:, 1, :].unsqueeze(2).to_broadcast([128, 16, 64]), ALU.mult,
               [b_xdt[s], b_ex[s]], [b_xdd[s]])
            if need_y(k):
                for h in range(16):
                    ACT(P, rhsA[s][:, h, :], masks[:, m_inc, :], AF.Copy, [b_masks, b_A[s]], [b_rA[s]], scale=A[s][:, h:h + 1])

        items = [(k, g) for k in range(nsteps) for g in range(4)]
        N = len(items)

        def info(n):
            k, g = items[n]
            d, ch = steps[k]
            return k, g, d, ch, k % NLD, k % NPR, n % NU, n % 2, slice(4 * g, 4 * g + 4)

        def A0(n):
            k, g, d, ch, sl, s, u, p2, hs = info(n)
            if not need_y(k):
                return
            m_inc, m_rest, m_D = mset(d)
            MM(P, pD[p2][:], masks[:, m_D, :], rhsA[s][:, hs, :], True, True, [b_masks, b_rA[s]], [b_pD[p2]])
            MM(P, pG[p2][:], btc[sl][:, g, :], ctc[sl][:, g, :], True, True, [b_bt[sl], b_ct[sl]], [b_pG[p2]])

        def A1(n):
            k, g, d, ch, sl, s, u, p2, hs = info(n)
            if not need_y(k):
                return
            m_inc, m_rest, m_D = mset(d)
            ACT(P, expD[u][:], pD[p2][:].rearrange("p (h l) -> p h l", h=4), AF.Exp, [b_pD[p2]], [b_eD[u]])
            TT(P, "dve", Gm[u][:], pG[p2][:], masks[:, m_inc, :], ALU.mult, [b_pG[p2], b_masks], [b_Gm[u]])

        def A2(n):
            k, g, d, ch, sl, s, u, p2, hs = info(n)
            if not need_y(k):
                return
            TT(P, "dve", MT[u][:], expD[u][:], Gm[u][:].unsqueeze(1).to_broadcast([128, 4, 128]), ALU.mult,
               [b_eD[u], b_Gm[u]], [b_MT[u]])

        def B0(n):
            k, g, d, ch, sl, s, u, p2, hs = info(n)
            if need_y(k):
                for hh in range(4):
                    MM(P, pY[p2][:, hh * 64:(hh + 1) * 64], MT[u][:, hh, :], xdt[s][:, 4 * g + hh, :], True, True,
                       [b_MT[u], b_xdt[s]], [b_pY[p2]])
                MM(P, pY[p2][:, 256:512], ctc[sl][:, g, :], Sbf[d][g][:], True, True, [b_ct[sl], b_Sbf[d][g]], [b_pY[p2]])
            MM(P, pS[:], bc[sl][:, g * 128:(g + 1) * 128], xdtd[s][:, hs, :], True, True, [b_b[sl], b_xdd[s]], [b_pS])

        def B1(n):
            k, g, d, ch, sl, s, u, p2, hs = info(n)
            S3 = S32[d][g][:].rearrange("p (h q) -> p h q", h=4)
            TT(P, "dve", S3, S3, ex[s][:, 2, hs].unsqueeze(2).to_broadcast([128, 4, 64]), ALU.mult,
               [b_S32[d][g], b_ex[s]], [b_S32[d][g]])
            TT(P, "dve", S32[d][g][:], pS[:], S32[d][g][:], ALU.add, [b_pS, b_S32[d][g]], [b_S32[d][g]])
            ACT(P, Sbf[d][g][:], S32[d][g][:], AF.Copy, [b_S32[d][g]], [b_Sbf[d][g]])
            if need_y(k):
                ys_ = k % 2
                TT(P, "dve", ytmp[p2][:], pY[p2][:, 256:512].rearrange("p (h q) -> p h q", h=4),
                   ex[s][:, 0, hs].unsqueeze(2).to_broadcast([128, 4, 64]), ALU.mult, [b_pY[p2], b_ex[s]], [b_yt[p2]])
                TT(P, "dve", ysum[ys_][:, g * 256:(g + 1) * 256], pY[p2][:, 0:256], ytmp[p2][:].rearrange("p h q -> p (h q)"),
                   ALU.add, [b_pY[p2], b_yt[p2]], [b_ys[ys_]])
                if g == 3:
                    DMA(P, "sp", C.Y[d, ch * 128:(ch + 1) * 128, :], ysum[ys_][:], r=[b_ys[ys_]], key="f2_ys%d" % ys_)

        load(0)
        load(1)
        stepprep(0)
        for i in range(N + 4):
            if i < N:
                k, g = items[i]
                if g == 0:
                    if k + 2 < nsteps:
                        load(k + 2)
                    if k + 1 < nsteps:
                        stepprep(k + 1)
                A0(i)
            if 0 <= i - 1 < N:
                A1(i - 1)
            if 0 <= i - 2 < N:
                A2(i - 2)
            if 0 <= i - 4 < N:
                B1(i - 4)
            if 0 <= i - 3 < N:
                B0(i - 3)
    P.fence()


class FinWork:
    def __init__(self, C, l):
        self.C, self.l = C, l
        self.ntt = NTT if C.upd[l] else 16

    def alloc(self, st):
        C, nc, P, l = self.C, self.C.nc, self.C.P, self.l
        self.dfull = sb(st, nc, "f3_dfull", [128, 1024], F32)
        self.snw = sb(st, nc, "f3_snw", [128, 8], F32)
        self.y0 = [sb(st, nc, "f3_y0%d" % i, [128, 1024], F32) for i in range(2)]
        self.y1 = [sb(st, nc, "f3_y1%d" % i, [128, 1024], F32) for i in range(1)]
        self.xs = [sb(st, nc, "f3_xs%d" % i, [128, 1024], BF16) for i in range(2)]
        self.zs = [sb(st, nc, "f3_zs%d" % i, [128, 1024], BF16) for i in range(2)]
        self.tx = [sb(st, nc, "f3_tx%d" % i, [128, 1024], F32) for i in range(1)]
        self.ss = [sb(st, nc, "f3_ss%d" % i, [128, 1], F32) for i in range(2)]
        self.rs = [sb(st, nc, "f3_rs%d" % i, [128, 1], F32) for i in range(2)]
        self.yn = [sb(st, nc, "f3_yn%d" % i, [128, 1024], BF16) for i in range(4)]
        self.gst = [sb(st, nc, "f3_gs%d" % i, [128, 8, 128], BF16) for i in range(2)]
        self.ptr = [ps(st, nc, "f3_pt%d" % i, [128, 8, 128], BF16) for i in range(2)]
        mk = lambda n, k: [Buf("%s%d" % (n, i)) for i in range(k)]
        self.b_y0, self.b_y1, self.b_xs, self.b_zs, self.b_tx = mk("y0", 2), mk("y1", 1), mk("xs", 2), mk("zs", 2), mk("tx", 1)
        self.b_ss, self.b_rs, self.b_yn, self.b_gs, self.b_pt = mk("ss", 2), mk("rs", 2), mk("yn", 4), mk("gs", 2), mk("pt", 2)
        self.b_df, self.b_snw = Buf("dfull"), Buf("snw")
        DMA(P, "sp", self.dfull[:], C.d_dfull[l].partition_broadcast(128), w=[self.b_df], key="f3_dfull")
        DMA(P, "sp", self.snw[:], C.d_snormT[l], w=[self.b_snw], key="f3_snw")
        self.load(0)
        self.load_y1(0)

    def load(self, t):
        C, P = self.C, self.C.P
        s = t % 2
        r0 = t * 128
        DMA(P, "sp", self.y0[s][:], C.Y[0, r0:r0 + 128, :], w=[self.b_y0[s]], key="f3_y0%d" % s)
        DMA(P, "sp", self.xs[s][:], C.XS[r0:r0 + 128, :], w=[self.b_xs[s]], key="f3_xs%d" % s)
        DMA(P, "sp", self.zs[s][:], C.TM_zs[r0:r0 + 128, :], w=[self.b_zs[s]], key="f3_zs%d" % s)

    def load_y1(self, t):
        C, P = self.C, self.C.P
        r0 = t * 128
        DMA(P, "sp", self.y1[0][:], C.Y[1, r0:r0 + 128, :], w=[self.b_y1[0]], key="f3_y10")

    def chain(self, t):
        C, P = self.C, self.C.P
        if t + 1 < self.ntt:
            self.load(t + 1)
        s, q = t % 2, t % 4
        y0, y1, xs, zs, tx, yn = self.y0[s], self.y1[0], self.xs[s], self.zs[s], self.tx[0], self.yn[q]
        by0, by1, bxs, bzs, btx, byn = self.b_y0[s], self.b_y1[0], self.b_xs[s], self.b_zs[s], self.b_tx[0], self.b_yn[q]
        TT(P, "pool", tx[:], xs[:], self.dfull[:], ALU.mult, [bxs, self.b_df], [btx])
        TT(P, "dve", y0[:], y0[:], y1[:], ALU.add, [by0, by1], [by0])
        if t + 1 < self.ntt:
            self.load_y1(t + 1)
        TT(P, "dve", y0[:], y0[:], tx[:], ALU.add, [by0, btx], [by0])
        TT(P, "dve", y0[:], y0[:], zs[:], ALU.mult, [by0, bzs], [by0])
        ACT(P, yn[:], y0[:], AF.Square, [by0], [byn, self.b_ss[s]], accum_out=self.ss[s][:])
        ACT(P, self.rs[s][:], self.ss[s][:], AF.Sqrt, [self.b_ss[s]], [self.b_rs[s]], scale=1.0 / 1024, bias=EPS)
        RECIP(P, self.rs[s][:], self.rs[s][:], [self.b_rs[s]], [self.b_rs[s]])
        rs = self.rs[s]
        P.add("dve", lambda e: e.tensor_scalar_mul(out=yn[:], in0=y0[:], scalar1=rs[:, 0:1]), [by0, self.b_rs[s]], [byn])

    def finish(self, t):
        C, P = self.C, self.C.P
        s, q = t % 2, t % 4
        yn, ptr, gst = self.yn[q], self.ptr[s], self.gst[s]
        for m in range(8):
            TR(P, ptr[:, m, :], yn[:, m * 128:(m + 1) * 128], C.ident[:], [self.b_yn[q]], [self.b_pt[s]])
        for m in range(8):
            ACT(P, gst[:, m, :], ptr[:, m, :], AF.Copy, [self.b_pt[s], self.b_snw], [self.b_gs[s]], scale=self.snw[:, m:m + 1])
        DMA(P, "sp", C.GsT[:, t * 128:(t + 1) * 128].rearrange("(m p) t -> p m t", p=128), gst[:], r=[self.b_gs[s]],
            key="f3_gs%d" % s)


def phase_merge(C, l):
    nc, P = C.nc, C.P
    upd = C.upd[l]
    last = (l == DEPTH - 1)
    tblk = TBLK if upd else TBLK[:4]
    ntt = NTT if upd else 16
    with ExitStack() as st:
        mixT = sb(st, nc, "g_mix", [128, 16, NT], BF16)
        with ExitStack() as s1:
            wbs = [sb(s1, nc, "g_wb%d" % i, [128, 8, DM], BF16) for i in range(2)]
            Gx = [sb(s1, nc, "g_gx%d" % i, [128, 8, 512], BF16) for i in range(2)]
            gate = [sb(s1, nc, "g_gate%d" % i, [128, 16, 512], BF16) for i in range(2)]
            tmp = [sb(s1, nc, "g_tmp%d" % i, [128, 512], F32) for i in range(3)]
            ring = Ring([ps(s1, nc, "g_p1_%d" % i, [128, 512]) for i in range(6)], "g1")
            b_wbs = [Buf("wb0"), Buf("wb1")]
            b_mixt = [Buf("mix%d" % i) for i in range(16)]
            b_gx = [Buf("gx0"), Buf("gx1")]
            b_gate = [Buf("gate0"), Buf("gate1")]
            b_tmp = [Buf("tmp%d" % i) for i in range(3)]
            branches = [(C.d_wbna, C.GaT, R_GNA), (C.d_wbfour, C.GfT, R_GF), (C.d_wbssd, C.GsT, R_GS)]
            li = 0
            ki = 0
            def loadwb(bi):
                DMA(P, "pool", wbs[bi % 2][:], branches[bi][0][l].rearrange("(c p) n -> p c n", p=128), w=[b_wbs[bi % 2]],
                    key="g_wb%d" % (bi % 2))

            loadwb(0)
            for bi, (dw, GT_, rg) in enumerate(branches):
                if bi + 1 < len(branches):
                    loadwb(bi + 1)
                wb, b_wb = wbs[bi % 2], b_wbs[bi % 2]

                def loadblk(k, GT_=GT_, rg=rg):
                    t0, tsz = tblk[k]
                    s_ = (li + k) % 2
                    DMA(P, "sp", Gx[s_][:, :, 0:tsz], GT_[:, t0:t0 + tsz].rearrange("(c p) t -> p c t", p=128), w=[b_gx[s_]],
                        key="g_gx%d" % s_)
                    DMA(P, "sp", gate[s_][:, :, 0:tsz], C.PT[rg:rg + DM, t0:t0 + tsz].rearrange("(c p) t -> p c t", p=128),
                        w=[b_gate[s_]], key="g_gate%d" % s_)

                loadblk(0)
                for k, (t0, tsz) in enumerate(tblk):
                    if k + 1 < len(tblk):
                        loadblk(k + 1)
                    s_ = (li + k) % 2
                    for dt_ in range(16):
                        pk, bp, _ = ring.next()
                        for c in range(8):
                            MM(P, pk[:, 0:tsz], wb[:, c, dt_ * 128:(dt_ + 1) * 128], Gx[s_][:, c, 0:tsz], c == 0, c == 7,
                               [b_wb, b_gx[s_]], [bp])
                        dst = mixT[:, dt_, t0:t0 + tsz]
                        if bi == 0:
                            TT(P, "dve", dst, pk[:, 0:tsz], gate[s_][:, dt_, 0:tsz], ALU.mult, [bp, b_gate[s_]], [b_mixt[dt_]])
                        else:
                            k_ = ki % 3
                            ki += 1
                            TT(P, "dve", tmp[k_][:, 0:tsz], pk[:, 0:tsz], gate[s_][:, dt_, 0:tsz], ALU.mult,
                               [bp, b_gate[s_]], [b_tmp[k_]])
                            TT(P, "pool" if dt_ % 2 else "dve", dst, dst, tmp[k_][:, 0:tsz], ALU.add,
                               [b_tmp[k_], b_mixt[dt_]], [b_mixt[dt_]])
                li += len(tblk)
        P.fence()
        with ExitStack() as s2:
            wo = sb(s2, nc, "g_wo", [128, 16, DM], BF16)
            xt = [sb(s2, nc, "g_xt%d" % i, [128, DM], F32) for i in range(2)]
            tmp = [sb(s2, nc, "g_t2%d" % i, [128, 512], F32) for i in range(2)]
            fnw = sb(s2, nc, "g_fnw", [128, DM], F32)
            junk = sb(s2, nc, "g_junk", [128, DM], BF16)
            ss = [sb(s2, nc, "g_ss%d" % i, [128, 1], F32) for i in range(2)]
            rs = [sb(s2, nc, "g_rs%d" % i, [128, 1], F32) for i in range(2)]
            ring = Ring([ps(s2, nc, "g_p2_%d" % i, [128, 512]) for i in range(6)], "g2")
            b_wo, b_fnw, b_junk = Buf("wo"), Buf("fnw"), Buf("junk")
            b_xt = [Buf("xt0"), Buf("xt1")]
            b_tmp = [Buf("t20"), Buf("t21")]
            b_ss = [Buf("ss0"), Buf("ss1")]
            b_rs = [Buf("rs0"), Buf("rs1")]
            DMA(P, "pool", wo[:], C.d_wout[l].rearrange("(c p) n -> p c n", p=128), w=[b_wo], key="g_wo")
            if last:
                DMA(P, "sp", fnw[:], C.d_fnorm.partition_broadcast(128), w=[b_fnw], key="g_fnw")

            def src_of(t):
                return C.xl_src[l][t * 128:(t + 1) * 128, :] if t < 16 else C.xc_src[l][(t - 16) * 128:(t - 15) * 128, :]

            def load(t):
                DMA(P, "sp", xt[t % 2][:], src_of(t), w=[b_xt[t % 2]], key="g_xt%d" % (t % 2))

            load(0)
            ki = 0
            for t in range(ntt):
                if t + 1 < ntt:
                    load(t + 1)
                s_ = t % 2
                gb = C.glb if t < 16 else C.gcb
                for db in range(4):
                    pk, bp, _ = ring.next()
                    for c in range(16):
                        MM(P, pk[:], mixT[:, c, t * 128:(t + 1) * 128], wo[:, c, db * 512:(db + 1) * 512], c == 0, c == 15,
                           [b_wo], [bp])
                    k_ = ki % 2
                    ki += 1
                    TT(P, "dve", tmp[k_][:], pk[:], gb[:, db * 512:(db + 1) * 512], ALU.mult, [bp, C.b_mod], [b_tmp[k_]])
                    TT(P, "pool", xt[s_][:, db * 512:(db + 1) * 512], xt[s_][:, db * 512:(db + 1) * 512], tmp[k_][:], ALU.add,
                       [b_tmp[k_], b_xt[s_]], [b_xt[s_]])
                if not last:
                    dst = C.XL[t * 128:(t + 1) * 128, :] if t < 16 else C.XC[(t - 16) * 128:(t - 15) * 128, :]
                    DMA(P, "sp", dst, xt[s_][:], r=[b_xt[s_]], key="g_xo%d" % s_)
                else:
                    ACT(P, junk[:], xt[s_][:], AF.Square, [b_xt[s_]], [b_junk, b_ss[s_]], accum_out=ss[s_][:])
                    ACT(P, rs[s_][:], ss[s_][:], AF.Sqrt, [b_ss[s_]], [b_rs[s_]], scale=1.0 / DM, bias=EPS)
                    RECIP(P, rs[s_][:], rs[s_][:], [b_rs[s_]], [b_rs[s_]])
                    STT(P, "dve", xt[s_][:], xt[s_][:], rs[s_][:, 0:1], fnw[:], ALU.mult, ALU.mult,
                        [b_xt[s_], b_rs[s_], b_fnw], [b_xt[s_]])
                    DMA(P, "sp", C.d_out[t * 128:(t + 1) * 128, :], xt[s_][:], r=[b_xt[s_]], key="g_xo%d" % s_)
    P.fence()


LATER_PHASES.extend([("attn", phase_attn), ("ssdprep", phase_ssd_prep), ("ssdscan", phase_ssd_scan),
                     ("four", phase_four), ("merge", phase_merge)])


def _bf16(a):
    import ml_dtypes
    return np.ascontiguousarray(a).astype(ml_dtypes.bfloat16)


def _consts():
    k = np.arange(2048, dtype=np.int64)
    m = (k[:, None] * k[None, :]) % 2048
    ang = 2.0 * np.pi * m.astype(np.float64) / 2048.0
    dftc = _bf16(np.cos(ang))
    dfts = _bf16(np.sin(ang))
    k2 = np.arange(256, dtype=np.int64)
    a2 = 2.0 * np.pi * ((k2[:, None] * k2[None, :]) % 256).astype(np.float64) / 256.0
    cs1 = _bf16(np.concatenate([np.cos(a2), -np.sin(a2)], axis=1))
    cs2 = _bf16(np.concatenate([np.cos(a2), np.sin(a2)], axis=1))
    pos = np.arange(NL)
    inv = (np.float32(10000.0) ** (-np.arange(32, dtype=np.float32) / np.float32(32))).astype(np.float32)
    row = (pos // 64).astype(np.float32)
    col = (pos % 64).astype(np.float32)
    ropec = np.zeros((128, NL), np.float32)
    ropes = np.zeros((128, NL), np.float32)
    for n in range(128):
        p = row if n < 64 else col
        a = (p * inv[n % 32]).astype(np.float32)
        ropec[n] = np.cos(a).astype(np.float32)
        ropes[n] = np.sin(a).astype(np.float32)
    rot = np.zeros((128, 128), np.float32)
    for mm_ in range(128):
        if (mm_ % 64) < 32:
            rot[mm_ + 32, mm_] = -1.0
        else:
            rot[mm_ - 32, mm_] = 1.0
    kk = np.arange(128)
    le = (kk[:, None] <= kk[None, :]).astype(np.float32)
    gt = (kk[:, None] > kk[None, :]).astype(np.float32)
    ge = (kk[:, None] >= kk[None, :]).astype(np.float32)
    lt = (kk[:, None] < kk[None, :]).astype(np.float32)
    masks = np.ascontiguousarray(np.stack([le, gt, ge, lt], axis=1))
    return dict(ident=_bf16(np.eye(128, dtype=np.float32)), dftc=dftc, dfts=dfts, cs1=cs1, cs2=cs2,
                ropec=ropec, ropes=ropes, rot=_bf16(rot), masks=masks)


NEG = -80.0


def _expand_rpb(rpb):
    GW, WR, WC = 64, 8, 16
    rows = NL // GW
    qc = np.arange(GW)
    qstart = np.clip(qc - WC // 2, 0, GW - WC)
    kc = np.arange(GW)
    colmask = (kc[:, None] >= qstart[None, :]) & (kc[:, None] < qstart[None, :] + WC)
    dc = np.clip(kc[:, None] - qc[None, :] + WC - 1, 0, 2 * WC - 2)
    combos = [(8, 8 + d) for d in (-2, -1, 0, 1, 2)]
    for i in (0, 1):
        combos += [(i, a) for a in range(4)]
    for i in (14, 15):
        combos += [(i, a) for a in range(12, 16)]
    out = np.full((8, 128, NBT, 128), NEG, np.float32)
    for ti, (i, a) in enumerate(combos):
        for kr_ in range(2):
            kr = 2 * a + kr_
            for r_ in range(2):
                r = 2 * i + r_
                start = min(max(r - WR // 2, 0), rows - WR)
                if not (start <= kr < start + WR):
                    continue
                dr = kr - r + WR - 1
                blk = np.where(colmask[None], rpb[:, dr][:, dc], np.float32(NEG))
                out[:, kr_ * 64:(kr_ + 1) * 64, ti, r_ * 64:(r_ + 1) * 64] = blk
    return out


def prep_inputs(inp):
    f = lambda a: np.ascontiguousarray(a, dtype=np.float32)
    sh = {}
    sh["w_ada"] = f(inp["w_ada"])
    sh["bada"] = f(inp["b_ada"]).reshape(DEPTH, 1, 6144)
    sh["badaT"] = f(inp["b_ada"].reshape(DEPTH, 48, 128).transpose(0, 2, 1))
    sh["normwT"] = f(inp["norm_w"].reshape(DEPTH, 16, 128).transpose(0, 2, 1))
    sh["w_in"] = f(inp["w_in"])
    sh["rpb"] = f(np.stack([_expand_rpb(inp["na_rpb"][l]) for l in range(DEPTH)]))
    for k in ("four_w", "wb_na", "wb_four", "wb_ssd", "w_out"):
        sh[k] = f(inp[k])
    sh["convw"] = f(inp["ssd_conv_w"].reshape(DEPTH, 7, 16, 128).transpose(0, 3, 2, 1))
    sh["convb"] = f(inp["ssd_conv_b"].reshape(DEPTH, 16, 128).transpose(0, 2, 1))
    sh["dtbias"] = f(inp["ssd_dt_bias"]).reshape(DEPTH, 1, 32)
    sh["alog"] = f(inp["ssd_a_log"]).reshape(DEPTH, 1, 32)
    sh["dfull"] = f(np.repeat(inp["ssd_d"], 64, axis=1)).reshape(DEPTH, 1, 1024)
    sh["snormT"] = f(inp["ssd_norm_w"].reshape(DEPTH, 8, 128).transpose(0, 2, 1))
    sh["fnorm"] = f(inp["final_norm_w"]).reshape(1, DM)
    sh.update(_consts())
    maps = []
    for b in range(inp["x"].shape[0]):
        m = dict(sh)
        m["x"] = f(inp["x"][b])
        m["ctx"] = f(inp["ctx"][b])
        m["cc"] = f(np.stack([inp["c"][b].reshape(16, 128).T, inp["c_ctx"].reshape(16, 128).T], axis=2))
        maps.append(m)
    return maps


def build(debug=False, stop_after=None):
    import ml_dtypes
    nc = bass.Bass("TRN2", target_bir_lowering=False)
    C = Ctx()
    C.nc = nc
    C.P = P = Prog(nc)
    C.upd = [True, False]
    C.later_phases = list(LATER_PHASES)

    def din(name, shape, dt=F32):
        return nc.dram_tensor(name, list(shape), dt, kind="ExternalInput").ap()

    def scr(name, shape, dt):
        return nc.dram_tensor(name, list(shape), dt, kind=("ExternalOutput" if debug else "Internal")).ap()

    C.d_x = din("x", [NL, DM])
    C.d_ctx = din("ctx", [NCX, DM])
    C.d_cc = din("cc", [128, 16, 2])
    C.d_wada = din("w_ada", [DEPTH, DM, 6144])
    C.d_bada = din("bada", [DEPTH, 1, 6144])
    C.d_badaT = din("badaT", [DEPTH, 128, 48])
    C.d_normwT = din("normwT", [DEPTH, 128, 16])
    C.d_win = din("w_in", [DEPTH, DM, INW])
    C.d_rpb = din("rpb", [DEPTH, 8, 128, NBT, 128])
    C.d_fourw = din("four_w", [DEPTH, 1024, 1024])
    C.d_wbna = din("wb_na", [DEPTH, 1024, DM])
    C.d_wbfour = din("wb_four", [DEPTH, 1024, DM])
    C.d_wbssd = din("wb_ssd", [DEPTH, 1024, DM])
    C.d_wout = din("w_out", [DEPTH, DM, DM])
    C.d_convw = din("convw", [DEPTH, 128, 16, 7])
    C.d_convb = din("convb", [DEPTH, 128, 16])
    C.d_dtbias = din("dtbias", [DEPTH, 1, 32])
    C.d_alog = din("alog", [DEPTH, 1, 32])
    C.d_dfull = din("dfull", [DEPTH, 1, 1024])
    C.d_snormT = din("snormT", [DEPTH, 128, 8])
    C.d_fnorm = din("fnorm", [1, DM])
    C.d_ident = din("ident", [128, 128], BF16)
    C.d_dftc = din("dftc", [2048, 2048], BF16)
    C.d_dfts = din("dfts", [2048, 2048], BF16)
    C.d_cs1 = din("cs1", [256, 512], BF16)
    C.d_cs2 = din("cs2", [256, 512], BF16)
    C.d_ropec = din("ropec", [128, NL])
    C.d_ropes = din("ropes", [128, NL])
    C.d_rot = din("rot", [128, 128], BF16)
    C.d_masks = din("masks", [128, 4, 128])
    C.d_out = nc.dram_tensor("out", [NL, DM], F32, kind="ExternalOutput").ap()

    C.XL = scr("XL", [NL, DM], F32)
    C.XC = scr("XC", [NCX, DM], F32)
    C.PT = scr("PT", [PT_ROWS, NT], BF16)
    C.TM_v = scr("TM_v", [NT, 1024], BF16)
    C.TM_zs = scr("TM_zs", [NT, 1024], BF16)
    C.TM_dt = scr("TM_dt", [NT, 32], F32)
    C.GaT = scr("GaT", [1024, NT], BF16)
    C.GfT = scr("GfT", [1024, NT], BF16)
    C.GsT = scr("GsT", [1024, NT], BF16)
    C.XS = scr("XS", [NT, 1024], BF16)
    C.Btm = scr("Btm", [NT, 512], BF16)
    C.BT = scr("BT", [512, NT], BF16)
    C.CT = scr("CT", [512, NT], BF16)
    C.Y = scr("Y", [2, NT, 1024], F32)
    C.xl_src = [C.d_x, C.XL]
    C.xc_src = [C.d_ctx, C.XC]

    phases = C.phases = []

    def done(name):
        phases.append(name)
        return stop_after is not None and name == stop_after

    with ExitStack() as top:
        C.ident = sb(top, nc, "ident", [128, 128], BF16)
        C.w1 = sb(top, nc, "w1", [128, 2, 16], F32)
        C.sh = sb(top, nc, "shv", [128, 2, 16], F32)
        C.glb = sb(top, nc, "glb", [128, DM], F32)
        C.gcb = sb(top, nc, "gcb", [128, DM], F32)
        C.b_mod = Buf("mod")
        DMA(P, "sp", C.ident[:], C.d_ident, key="ident")
        P.fence()
        stop = False
        for l in range(DEPTH):
            with ExitStack() as sth:
                C.hT = sb(sth, nc, "hT", [128, 16, NT], BF16)
                phase_modh(C, l)
                if done("h%d" % l):
                    break
                phase_gemm(C, l)
            if done("gemm%d" % l):
                break
            for name, fn in C.later_phases:
                fn(C, l)
                if done("%s%d" % (name, l)):
                    stop = True
                    break
            if stop:
                break
            P.new_epoch()
        P.emit(top)
    return nc, C


_CACHE = {}


def kernel(**inputs):
    maps = prep_inputs(inputs)
    if "nc" not in _CACHE:
        _CACHE["nc"] = build(debug=False)[0]
    nc = _CACHE["nc"]
    res = run_bass_kernel_spmd(nc, maps, core_ids=list(range(len(maps))))
    out = np.stack([np.asarray(r["out"], dtype=np.float32) for r in res.results], axis=0)
    return out
```

```python
import numpy as np
from contextlib import ExitStack
import concourse.bass as bass
import concourse.mybir as mybir
from concourse.bass_utils import run_bass_kernel_spmd

F32 = mybir.dt.float32
BF16 = mybir.dt.bfloat16
AF = mybir.ActivationFunctionType
ALU = mybir.AluOpType
AX = mybir.AxisListType


class Buf:
    __slots__ = ("name", "w", "r")

    def __init__(self, name):
        self.name = name
        self.w = None
        self.r = []


class Ins:
    __slots__ = ("eng", "fn", "deps", "dma", "count", "marked")

    def __init__(self, eng, fn, deps, dma):
        self.eng = eng
        self.fn = fn
        self.deps = deps
        self.dma = dma
        self.count = 0
        self.marked = False


class Prog:
    ENGS = ("sp", "act", "dve", "pool", "pe")

    def __init__(self, nc):
        self.nc = nc
        self.ins = []
        self.dma_cnt = {}
        self.epoch = 0
        self.epoch_of = []
        self.last_eng = {}
        self.last_dma = {}
        self.pending = {}
        self.keymap = {}

    def fence(self):
        self.keymap = {}
        f = set(self.last_eng.values()) | set(self.last_dma.values())
        for e in self.ENGS:
            self.pending[e] = set(f) | self.pending.get(e, set())

    def new_epoch(self):
        self.epoch += 1

    def add(self, eng, fn, reads=(), writes=(), dma=None):
        if dma is not None:
            if dma not in self.keymap:
                self.keymap[dma] = "k%d" % len(self.keymap)
            dma = self.keymap[dma]
        idx = len(self.ins)
        deps = set()
        for b in reads:
            if b.w is not None:
                deps.add(b.w)
        for b in writes:
            if b.w is not None:
                pw = self.ins[b.w]
                if not (pw.eng == eng and pw.dma is None and dma is None):
                    deps.add(b.w)
            deps.update(b.r)
        if eng in self.pending:
            deps |= self.pending.pop(eng)
        deps.discard(idx)
        if dma is not None:
            self.last_dma[dma] = idx
        else:
            self.last_eng[eng] = idx
        for b in reads:
            b.r.append(idx)
        for b in writes:
            b.w = idx
            b.r = []
        ins = Ins(eng, fn, deps, dma)
        if dma is not None:
            self.dma_cnt[dma] = self.dma_cnt.get(dma, 0) + 16
            ins.count = self.dma_cnt[dma]
        self.ins.append(ins)
        self.epoch_of.append(self.epoch)
        return idx

    def emit(self, stack):
        nc = self.nc
        ins = self.ins
        for i, it in enumerate(ins):
            for d in it.deps:
                dd = ins[d]
                if dd.dma is None:
                    if dd.eng == "pe" and it.eng == "pe":
                        continue
                    dd.marked = True
        cnt = {}
        semkeys = set()
        for i, it in enumerate(ins):
            if it.dma is None and it.marked:
                k = (it.eng, self.epoch_of[i])
                cnt[k] = cnt.get(k, 0) + 1
                it.count = cnt[k]
                semkeys.add(k)
        sems = {}
        for k in sorted(semkeys):
            sems[k] = stack.enter_context(nc.semaphore("s_%s_%d" % k))
        for k in sorted(self.dma_cnt):
            sems[("dma", k)] = stack.enter_context(nc.semaphore("d_" + k))

        def semof(d):
            dd = ins[d]
            if dd.dma is not None:
                return ("dma", dd.dma), dd.count
            return (dd.eng, self.epoch_of[d]), dd.count

        def emit_engine(engname, eng):
            waited = {}
            for i, it in enumerate(ins):
                if it.eng != engname:
                    continue
                for d in sorted(it.deps):
                    dd = ins[d]
                    if dd.dma is None and dd.eng == "pe" and engname == "pe":
                        continue
                    k, v = semof(d)
                    if waited.get(k, 0) >= v:
                        continue
                    eng.wait_ge(sems[k], v)
                    waited[k] = v
                bi = it.fn(eng)
                if it.dma is not None:
                    bi.then_inc(sems[("dma", it.dma)], 16)
                elif it.marked:
                    bi.then_inc(sems[(engname, self.epoch_of[i])], 1)
            if engname == "sp":
                for k in sorted(self.dma_cnt):
                    eng.wait_ge(sems[("dma", k)], self.dma_cnt[k])
                for k in sorted(cnt):
                    eng.wait_ge(sems[k], cnt[k])

        with nc.Block() as block:
            @block.sync
            def _(e):
                emit_engine("sp", e)

            @block.scalar
            def _(e):
                emit_engine("act", e)

            @block.vector
            def _(e):
                emit_engine("dve", e)

            @block.gpsimd
            def _(e):
                emit_engine("pool", e)

            @block.tensor
            def _(e):
                emit_engine("pe", e)


def DMA(P, q, out, in_, r=(), w=(), key=None):
    return P.add(q, lambda e: e.dma_start(out=out, in_=in_), r, w, dma=key)


def MM(P, out, lhsT, rhs, start, stop, r, w):
    return P.add("pe", lambda e: e.matmul(out, lhsT=lhsT, rhs=rhs, start=start, stop=stop), r, w)


def TR(P, out, in_, ident, r, w):
    return P.add("pe", lambda e: e.transpose(out=out, in_=in_, identity=ident), r, w)


def ACT(P, out, in_, func, r, w, **kw):
    return P.add("act", lambda e: e.activation(out=out, in_=in_, func=func, **kw), r, w)


def TT(P, eng, out, in0, in1, op, r, w):
    return P.add(eng, lambda e: e.tensor_tensor(out=out, in0=in0, in1=in1, op=op), r, w)


def TS(P, eng, out, in0, s1, s2, op0, op1, r, w):
    return P.add(eng, lambda e: e.tensor_scalar(out=out, in0=in0, scalar1=s1, scalar2=s2, op0=op0, op1=op1), r, w)


def STT(P, eng, out, in0, scalar, in1, op0, op1, r, w):
    return P.add(eng, lambda e: e.scalar_tensor_tensor(out=out, in0=in0, scalar=scalar, in1=in1, op0=op0, op1=op1), r, w)


def CP(P, eng, out, in_, r, w):
    return P.add(eng, lambda e: e.tensor_copy(out=out, in_=in_), r, w)


def RECIP(P, out, in_, r, w):
    return P.add("dve", lambda e: e.reciprocal(out=out, in_=in_), r, w)


def MEMSET(P, eng, ap, val, w):
    return P.add(eng, lambda e: e.memset(ap, val), (), w)


DM = 2048
NL = 2048
NCX = 256
NT = NL + NCX
NTT = NT // 128
DEPTH = 2
INW = 15392
EPS = 1e-6
C_Q, C_K, C_V, C_ZNA, C_UF, C_ZF, C_XBC, C_ZS, C_DT, C_GNA, C_GF, C_GS = (
    0, 1024, 2048, 3072, 4096, 5120, 6144, 8192, 9216, 9248, 11296, 13344)
R_Q, R_K, R_ZNA, R_UF, R_ZF, R_XBC, R_GNA, R_GF, R_GS = 0, 1024, 2048, 3072, 4096, 5120, 7168, 9216, 11264
PT_ROWS = 13312
TBLK = [(0, 512), (512, 512), (1024, 512), (1536, 512), (2048, 256)]
NBT = 21


class Ctx:
    pass


LATER_PHASES = []


_UID = [0]


def _uid(name):
    _UID[0] += 1
    return "%s_%d" % (name, _UID[0])


def sb(stack, nc, name, shape, dt):
    return stack.enter_context(nc.sbuf_tensor(_uid("sb_" + name), list(shape), dt))


def ps(stack, nc, name, shape, dt=F32):
    return stack.enter_context(nc.psum_tensor(_uid("ps_" + name), list(shape), dt))


class Ring:
    def __init__(self, tiles, prefix):
        self.tiles = tiles
        self.toks = [Buf("%s%d" % (prefix, i)) for i in range(len(tiles))]
        self.i = 0

    def next(self):
        k = self.i % len(self.tiles)
        self.i += 1
        return self.tiles[k], self.toks[k], k


def phase_modh(C, l):
    nc, P = C.nc, C.P
    upd = C.upd[l]
    with ExitStack() as st:
        cc = sb(st, nc, "ma_cc", [128, 16, 2], F32)
        s32 = sb(st, nc, "ma_s32", [128, 16, 2], F32)
        scc = sb(st, nc, "ma_scc", [128, 16, 2], BF16)
        repl = sb(st, nc, "ma_repl", [128, 16, 128], BF16)
        repc = sb(st, nc, "ma_repc", [128, 16, 128], BF16)
        badaT = sb(st, nc, "ma_badaT", [128, 48], F32)
        badag = sb(st, nc, "ma_badag", [128, 2048], F32)
        normwT = sb(st, nc, "ma_normw", [128, 16], F32)
        modT = sb(st, nc, "ma_modT", [128, 32, 2], F32)
        wblk = [sb(st, nc, "ma_w%d" % i, [128, 16, 512], BF16) for i in range(2)]
        pmod = ps(st, nc, "ma_pmod", [128, 32, 2])
        pg = [ps(st, nc, "ma_pg%d" % i, [128, 512]) for i in range(2)]
        b_cc, b_s32, b_scc, b_repl, b_repc = Buf("cc"), Buf("s32"), Buf("scc"), Buf("repl"), Buf("repc")
        b_badaT, b_badag, b_normw, b_modT, b_pmod = Buf("badaT"), Buf("badag"), Buf("normw"), Buf("modT"), Buf("pmod")
        b_w = [Buf("maw0"), Buf("maw1")]
        b_pg = [Buf("mapg0"), Buf("mapg1")]
        xt = [sb(st, nc, "ph_xt%d" % i, [128, DM], F32) for i in range(2)]
        xn = [sb(st, nc, "ph_xn%d" % i, [128, DM], BF16) for i in range(2)]
        junk = sb(st, nc, "ph_junk", [128, DM], BF16)
        ss = [sb(st, nc, "ph_ss%d" % i, [128, 1], F32) for i in range(2)]
        rs = [sb(st, nc, "ph_rs%d" % i, [128, 1], F32) for i in range(2)]
        tp = [ps(st, nc, "ph_tp%d" % i, [128, 16, 128], BF16) for i in range(2)]
        b_xt = [Buf("xt0"), Buf("xt1")]
        b_xn = [Buf("xn0"), Buf("xn1")]
        b_ss = [Buf("ss0"), Buf("ss1")]
        b_rs = [Buf("rs0"), Buf("rs1")]
        b_tp = [Buf("tp0"), Buf("tp1")]
        b_junk = Buf("junk")

        def loadx(t):
            s = t % 2
            src = C.xl_src[l][t * 128:(t + 1) * 128, :] if t < 16 else C.xc_src[l][(t - 16) * 128:(t - 15) * 128, :]
            DMA(P, "sp", xt[s][:], src, w=[b_xt[s]], key="ph_xt%d" % s)

        def htile(t):
            if t + 1 < NTT:
                loadx(t + 1)
            s = t % 2
            ACT(P, junk[:], xt[s][:], AF.Square, [b_xt[s]], [b_junk, b_ss[s]], accum_out=ss[s][:])
            ACT(P, rs[s][:], ss[s][:], AF.Sqrt, [b_ss[s]], [b_rs[s]], scale=1.0 / DM, bias=EPS)
            RECIP(P, rs[s][:], rs[s][:], [b_rs[s]], [b_rs[s]])
            P.add("dve", lambda e, s=s: e.tensor_scalar_mul(out=xn[s][:], in0=xt[s][:], scalar1=rs[s][:, 0:1]),
                  [b_xt[s], b_rs[s]], [b_xn[s]])
            for j in range(16):
                TR(P, tp[s][:, j, :], xn[s][:, j * 128:(j + 1) * 128], C.ident[:], [b_xn[s]], [b_tp[s]])
            out = C.hT[:, :, t * 128:(t + 1) * 128]
            if t % 2 == 0:
                ACT(P, out, tp[s][:], AF.Copy, [b_tp[s]], [])
            else:
                CP(P, "dve", out, tp[s][:], [b_tp[s]], [])

        DMA(P, "sp", cc[:], C.d_cc, w=[b_cc], key="ma_cc")
        DMA(P, "sp", badaT[:], C.d_badaT[l], w=[b_badaT], key="ma_badaT")
        DMA(P, "sp", badag[:], C.d_bada[l][:, 4096:6144].partition_broadcast(128), w=[b_badag], key="ma_badag")
        DMA(P, "sp", normwT[:], C.d_normwT[l], w=[b_normw], key="ma_normw")
        loadx(0)
        ACT(P, s32[:], cc[:], AF.Silu, [b_cc], [b_s32])
        CP(P, "dve", scc[:], s32[:], [b_s32], [b_scc])
        CP(P, "dve", repl[:], s32[:, :, 0:1].to_broadcast([128, 16, 128]), [b_s32], [b_repl])
        if upd:
            CP(P, "dve", repc[:], s32[:, :, 1:2].to_broadcast([128, 16, 128]), [b_s32], [b_repc])
        wv = C.d_wada[l].rearrange("(j p) n -> p j n", p=128)

        def load(b):
            s = b % 2
            DMA(P, "pool", wblk[s][:], wv[:, :, b * 512:(b + 1) * 512], w=[b_w[s]], key="ma_w%d" % s)

        load(0)
        k = 0
        tnext = 0
        for b in range(12):
            if b + 1 < 12:
                load(b + 1)
            for _ in range(2 if b % 2 == 0 else 1):
                htile(tnext)
                tnext += 1
            s = b % 2
            if b < 8:
                for nt in range(4):
                    n = b * 4 + nt
                    for j in range(16):
                        MM(P, pmod[:, n, :], wblk[s][:, j, nt * 128:(nt + 1) * 128], scc[:, j, :], j == 0, j == 15,
                           [b_w[s], b_scc], [b_pmod])
            else:
                gb = b - 8
                for which in ([0, 1] if upd else [0]):
                    rep = repl if which == 0 else repc
                    brep = b_repl if which == 0 else b_repc
                    pgt, bpg = pg[k % 2], b_pg[k % 2]
                    k += 1
                    for j in range(16):
                        MM(P, pgt[:], rep[:, j, :], wblk[s][:, j, :], j == 0, j == 15, [b_w[s], brep], [bpg])
                    dst = (C.glb if which == 0 else C.gcb)[:, gb * 512:(gb + 1) * 512]
                    TT(P, "dve", dst, pgt[:], badag[:, gb * 512:(gb + 1) * 512], ALU.add, [bpg, b_badag], [C.b_mod])
        assert tnext == NTT
        TT(P, "dve", modT[:], pmod[:], badaT[:, 0:32].unsqueeze(2).to_broadcast([128, 32, 2]), ALU.add,
           [b_pmod, b_badaT], [b_modT])
        for which in range(2):
            STT(P, "dve", C.w1[:, which, :], modT[:, 16:32, which], 1.0, normwT[:], ALU.add, ALU.mult,
                [b_modT, b_normw], [C.b_mod])
            CP(P, "dve", C.sh[:, which, :], modT[:, 0:16, which], [b_modT], [C.b_mod])
        P.fence()
        for j in range(16):
            for wh, (c0, c1) in enumerate([(0, NL), (NL, NT)]):
                v = C.hT[:, j, c0:c1]
                if j % 2 == 0:
                    ACT(P, v, v, AF.Identity, [C.b_mod], [], scale=C.w1[:, wh, j:j + 1], bias=C.sh[:, wh, j:j + 1])
                else:
                    STT(P, "dve", v, v, C.w1[:, wh, j:j + 1], C.sh[:, wh, j:j + 1].to_broadcast([128, c1 - c0]),
                        ALU.mult, ALU.add, [C.b_mod], [])
    P.fence()


def phase_gemm(C, l):
    nc, P = C.nc, C.P
    with ExitStack() as st:
        NWB = 3
        wblk = [sb(st, nc, "pc_w%d" % i, [128, 16, 512], BF16) for i in range(NWB)]
        ost = [sb(st, nc, "pc_o%d" % i, [128, NT], BF16) for i in range(3)]
        tst = [sb(st, nc, "pc_t%d" % i, [128, NTT, 512], BF16) for i in range(2)]
        wdt = sb(st, nc, "pc_wdt", [128, 16, 32], BF16)
        dtst = sb(st, nc, "pc_dtst", [128, NTT, 32], F32)
        dtb = sb(st, nc, "pc_dtb", [128, 32], F32)
        ring = Ring([ps(st, nc, "pc_p%d" % i, [128, 512]) for i in range(8)], "pcp")
        b_w = [Buf("pcw%d" % i) for i in range(NWB)]
        b_o = [Buf("pco%d" % i) for i in range(3)]
        b_t = [Buf("pct%d" % i) for i in range(2)]
        b_wdt, b_dtst, b_dtb = Buf("wdt"), Buf("dtst"), Buf("dtb")
        wv = C.d_win[l].rearrange("(j p) n -> p j n", p=128)

        fm = [(C_Q, R_Q, 1024, "q"), (C_K, R_K, 1024, "copy"), (C_ZNA, R_ZNA, 1024, "silu"), (C_UF, R_UF, 1024, "copy"),
              (C_ZF, R_ZF, 1024, "silu"), (C_XBC, R_XBC, 2048, "copy"), (C_GNA, R_GNA, 6144, "sigmoid")]
        blocks = []
        for (c0, r0, n, fn) in fm:
            for b in range(n // 512):
                blocks.append(("fm", c0 + b * 512, r0 + b * 512, fn))
        for b in range(2):
            blocks.append(("tm", C_V + b * 512, (C.TM_v, b), "copy"))
        for b in range(2):
            blocks.append(("tm", C_ZS + b * 512, (C.TM_zs, b), "silu"))

        def load(i):
            s = i % NWB
            c0 = blocks[i][1]
            DMA(P, "pool", wblk[s][:], wv[:, :, c0:c0 + 512], w=[b_w[s]], key="pc_w%d" % s)

        def evac(fn, out, in_, r, w):
            if fn == "copy":
                CP(P, "dve", out, in_, r, w)
            elif fn == "q":
                P.add("dve", lambda e: e.tensor_scalar_mul(out=out, in0=in_, scalar1=float(128 ** -0.5)), r, w)
            elif fn == "silu":
                ACT(P, out, in_, AF.Silu, r, w)
            elif fn == "sigmoid":
                ACT(P, out, in_, AF.Sigmoid, r, w)

        DMA(P, "sp", dtb[:], C.d_dtbias[l].partition_broadcast(128), w=[b_dtb], key="pc_dtb")
        DMA(P, "pool", wdt[:], wv[:, :, C_DT:C_DT + 32], w=[b_wdt], key="pc_wdt")
        load(0)
        load(1)
        oi = 0
        ti = 0
        for i, (kind, c0, dst, fn) in enumerate(blocks):
            if i + 2 < len(blocks):
                load(i + 2)
            s = i % NWB
            if kind == "fm":
                need_ctx = C.upd[l] or dst in (R_K, R_K + 512, R_XBC, R_XBC + 512, R_XBC + 1024, R_XBC + 1536)
                tbl = TBLK if need_ctx else TBLK[:4]
                ncol = NT if need_ctx else NL
                for nt in range(4):
                    o, bo = ost[oi % 3], b_o[oi % 3]
                    for (t0, tsz) in tbl:
                        pk, bp, _ = ring.next()
                        for j in range(16):
                            MM(P, pk[:, 0:tsz], wblk[s][:, j, nt * 128:(nt + 1) * 128], C.hT[:, j, t0:t0 + tsz],
                               j == 0, j == 15, [b_w[s]], [bp])
                        evac(fn, o[:, t0:t0 + tsz], pk[:, 0:tsz], [bp], [bo])
                    DMA(P, "sp", C.PT[dst + nt * 128:dst + (nt + 1) * 128, 0:ncol], o[:, 0:ncol], r=[bo], key="pc_o%d" % (oi % 3))
                    oi += 1
            else:
                tm, b = dst
                o, bo = tst[ti % 2], b_t[ti % 2]
                ntm = NTT if (C.upd[l] or fn == "copy") else 16
                for t in range(ntm):
                    pk, bp, _ = ring.next()
                    for j in range(16):
                        MM(P, pk[:], C.hT[:, j, t * 128:(t + 1) * 128], wblk[s][:, j, :], j == 0, j == 15, [b_w[s]], [bp])
                    evac(fn, o[:, t, :], pk[:], [bp], [bo])
                DMA(P, "sp", tm.rearrange("(t p) n -> p t n", p=128)[:, 0:ntm, b * 512:(b + 1) * 512], o[:, 0:ntm, :], r=[bo],
                    key="pc_t%d" % (ti % 2))
                ti += 1
        for t in range(NTT):
            pk, bp, _ = ring.next()
            for j in range(16):
                MM(P, pk[:, 0:32], C.hT[:, j, t * 128:(t + 1) * 128], wdt[:, j, :], j == 0, j == 15, [b_wdt], [bp])
            TT(P, "dve", dtst[:, t, :], pk[:, 0:32], dtb[:], ALU.add, [bp, b_dtb], [b_dtst])
        ACT(P, dtst[:], dtst[:], AF.Exp, [b_dtst], [b_dtst])
        ACT(P, dtst[:], dtst[:], AF.Ln, [b_dtst], [b_dtst], bias=1.0)
        DMA(P, "sp", C.TM_dt.rearrange("(t p) n -> p t n", p=128), dtst[:], r=[b_dtst], key="pc_dtst")
    P.fence()


def phase_attn(C, l):
    nc, P = C.nc, C.P
    upd = C.upd[l]
    npairs = 18 if upd else 16
    NS = 3
    with ExitStack() as st:
        QT = [sb(st, nc, "pd_q%d" % i, [128, NT], BF16) for i in range(2)]
        KT = [sb(st, nc, "pd_k%d" % i, [128, NT], BF16) for i in range(2)]
        ZT = [sb(st, nc, "pd_z%d" % i, [128, NT], BF16) for i in range(2)]
        V = [sb(st, nc, "pd_v%d" % i, [128, NTT, 128], BF16) for i in range(2)]
        Et = [sb(st, nc, "pd_e%d" % i, [128, NBT, 128], F32) for i in range(2)]
        gat = [sb(st, nc, "pd_g%d" % i, [128, NT], BF16) for i in range(2)]
        ones = sb(st, nc, "pd_ones", [128, 128], BF16)
        Pw = [sb(st, nc, "pd_pw%d" % i, [128, 5, 128], F32) for i in range(NS)]
        Pb = [sb(st, nc, "pd_pb%d" % i, [128, 7, 128], BF16) for i in range(NS)]
        rden = [sb(st, nc, "pd_rd%d" % i, [128, 128], F32) for i in range(2)]
        otmp = [sb(st, nc, "pd_ot%d" % i, [128, 128], F32) for i in range(2)]
        Sps = [ps(st, nc, "pd_s%d" % i, [128, 1024]) for i in range(NS)]
        OD = [ps(st, nc, "pd_od%d" % i, [128, 2, 128]) for i in range(2)]
        mk = lambda n, k: [Buf("%s%d" % (n, i)) for i in range(k)]
        b_q, b_k, b_z, b_v = mk("q", 2), mk("k", 2), mk("z", 2), mk("v", 2)
        b_E, b_gat = mk("E", 2), mk("gat", 2)
        b_Pw, b_Pbw, b_Pbc, b_S = mk("Pw", NS), mk("Pbw", NS), mk("Pbc", NS), mk("S", NS)
        b_rd, b_ot, b_OD = mk("rd", 2), mk("ot", 2), mk("OD", 2)
        b_ones = Buf("ones")
        MEMSET(P, "dve", ones[:], 1.0, [b_ones])
        vview = C.TM_v.rearrange("(t p) n -> p t n", p=128)

        def load(h):
            s = h % 2
            DMA(P, "sp", QT[s][:], C.PT[R_Q + h * 128:R_Q + (h + 1) * 128, :], w=[b_q[s]], key="pd_q%d" % s)
            DMA(P, "sp", KT[s][:], C.PT[R_K + h * 128:R_K + (h + 1) * 128, :], w=[b_k[s]], key="pd_k%d" % s)
            DMA(P, "sp", ZT[s][:], C.PT[R_ZNA + h * 128:R_ZNA + (h + 1) * 128, :], w=[b_z[s]], key="pd_z%d" % s)
            DMA(P, "sp", V[s][:], vview[:, :, h * 128:(h + 1) * 128], w=[b_v[s]], key="pd_v%d" % s)
            DMA(P, "sp", Et[s][:], C.d_rpb[l, h], w=[b_E[s]], key="pd_e%d" % s)

        def expE(h):
            s = h % 2
            ACT(P, Et[s][:], Et[s][:], AF.Exp, [b_E[s]], [b_E[s]])

        def geom(i):
            if i >= 16:
                alist, e0 = [], 0
            elif 2 <= i <= 13:
                alist, e0 = list(range(i - 2, i + 3)), 0
            elif i < 2:
                alist, e0 = [0, 1, 2, 3], 5 + 4 * i
            else:
                alist, e0 = [12, 13, 14, 15], 13 + 4 * (i - 14)
            return alist, e0

        items = [(h, i) for h in range(8) for i in range(npairs)]

        def stage_s(n):
            h, i = items[n]
            hs, p_ = h % 2, n % NS
            alist, e0 = geom(i)
            kt = alist + [16, 17]
            q0 = i * 128
            for si, a in enumerate(kt):
                MM(P, Sps[p_][:, si * 128:(si + 1) * 128], KT[hs][:, a * 128:(a + 1) * 128], QT[hs][:, q0:q0 + 128], True, True,
                   [b_q[hs], b_k[hs]], [b_S[p_]])

        def stage_p(n):
            h, i = items[n]
            hs, p_ = h % 2, n % NS
            alist, e0 = geom(i)
            nw = len(alist)
            nk = nw + 2
            Sp = Sps[p_]
            if nw:
                ACT(P, Pw[p_][:, 0:nw, :], Sp[:, 0:nw * 128].rearrange("p (a q) -> p a q", q=128), AF.Exp,
                    [b_S[p_]], [b_Pw[p_]])
                TT(P, "pool", Pb[p_][:, 0:nw, :], Pw[p_][:, 0:nw, :], Et[hs][:, e0:e0 + nw, :], ALU.mult,
                   [b_Pw[p_], b_E[hs]], [b_Pbw[p_]])
            ACT(P, Pb[p_][:, nw:nk, :], Sp[:, nw * 128:nk * 128].rearrange("p (a q) -> p a q", q=128), AF.Exp,
                [b_S[p_]], [b_Pbc[p_]])

        def stage_o(n):
            h, i = items[n]
            hs, p_, o_ = h % 2, n % NS, n % 2
            alist, e0 = geom(i)
            kt = alist + [16, 17]
            nk = len(kt)
            q0 = i * 128
            for si, a in enumerate(kt):
                MM(P, OD[o_][:, 0, :], V[hs][:, a, :], Pb[p_][:, si, :], si == 0, si == nk - 1,
                   [b_v[hs], b_Pbw[p_], b_Pbc[p_]], [b_OD[o_]])
            for si in range(nk):
                MM(P, OD[o_][:, 1, :], ones[:], Pb[p_][:, si, :], si == 0, si == nk - 1,
                   [b_ones, b_Pbw[p_], b_Pbc[p_]], [b_OD[o_]])
            RECIP(P, rden[o_][:], OD[o_][:, 1, :], [b_OD[o_]], [b_rd[o_]])
            TT(P, "dve", otmp[o_][:], OD[o_][:, 0, :], rden[o_][:], ALU.mult, [b_OD[o_], b_rd[o_]], [b_ot[o_]])
            TT(P, "dve", gat[hs][:, q0:q0 + 128], otmp[o_][:], ZT[hs][:, q0:q0 + 128], ALU.mult,
               [b_ot[o_], b_z[hs]], [b_gat[hs]])
            if i == npairs - 1:
                DMA(P, "sp", C.GaT[h * 128:(h + 1) * 128, 0:npairs * 128], gat[hs][:, 0:npairs * 128], r=[b_gat[hs]],
                    key="pd_g%d" % hs)

        load(0)
        expE(0)
        load(1)
        N = len(items)
        stage_s(0)
        if N > 1:
            stage_s(1)
        stage_p(0)
        for n in range(N):
            h, i = items[n]
            if i == 0 and h >= 1 and h + 1 < 8:
                load(h + 1)
            if i == npairs // 2 and h + 1 < 8:
                expE(h + 1)
            if n + 2 < N:
                stage_s(n + 2)
            if n + 1 < N:
                stage_p(n + 1)
            stage_o(n)
    P.fence()


def phase_four(C, l):
    nc, P = C.nc, C.P
    upd = C.upd[l]
    ntt = NTT if upd else 16
    with ExitStack() as st:
        UCS = sb(st, nc, "pe_ucs", [128, NTT, 4, 512], BF16)
        b_ucs = [Buf("ucsA"), Buf("ucsD")]
        with ExitStack() as s1:
            cs1 = sb(s1, nc, "pe_cs1", [128, 2, 512], BF16)
            ufT = [sb(s1, nc, "pe_uf%d" % i, [128, 2, NT], BF16) for i in range(2)]
            ring = Ring([ps(s1, nc, "pe_p1_%d" % i, [128, 512]) for i in range(4)], "pe1")
            b_cs1 = Buf("cs1")
            b_uf = [Buf("uf0"), Buf("uf1")]
            DMA(P, "sp", cs1[:], C.d_cs1.rearrange("(c p) n -> p c n", p=128), w=[b_cs1], key="pe_cs1")

            def loadu(g):
                DMA(P, "sp", ufT[g % 2][:], C.PT[R_UF + g * 256:R_UF + (g + 1) * 256, :].rearrange("(c p) t -> p c t", p=128),
                    w=[b_uf[g % 2]], key="pe_uf%d" % (g % 2))

            loadu(0)
            k = 0
            for g in range(4):
                if g + 1 < 4:
                    loadu(g + 1)
                for t in range(ntt):
                    pk, bp, _ = ring.next()
                    for c in range(2):
                        MM(P, pk[:], ufT[g % 2][:, c, t * 128:(t + 1) * 128], cs1[:, c, :], c == 0, c == 1,
                           [b_uf[g % 2], b_cs1], [bp])
                    if k % 2 == 0:
                        ACT(P, UCS[:, t, g, :], pk[:], AF.Copy, [bp], [b_ucs[0]])
                    else:
                        CP(P, "dve", UCS[:, t, g, :], pk[:], [bp], [b_ucs[1]])
                    k += 1
        P.fence()
        with ExitStack() as s2:
            DC = [sb(s2, nc, "pe_dc%d" % i, [128, 16, 256], BF16) for i in range(2)]
            DS = [sb(s2, nc, "pe_ds%d" % i, [128, 16, 256], BF16) for i in range(2)]
            cs2 = sb(s2, nc, "pe_cs2", [128, 2, 512], BF16)
            fw = sb(s2, nc, "pe_fw", [128, 8, 1024], BF16)
            FT = [sb(s2, nc, "pe_ft%d" % i, [128, 8, 256], BF16) for i in range(2)]
            Zf = [sb(s2, nc, "pe_zf%d" % i, [128, 8, 256], BF16) for i in range(2)]
            gst = [sb(s2, nc, "pe_gs%d" % i, [128, 8, 256], BF16) for i in range(2)]
            ring = Ring([ps(s2, nc, "pe_p2_%d" % i, [128, 256]) for i in range(6)], "pe2")
            b_tab = [Buf("tab0"), Buf("tab1")]
            b_tabs = [Buf("tabs0"), Buf("tabs1")]
            b_cs2, b_fw = Buf("cs2"), Buf("fw")
            b_ft = [Buf("ft0"), Buf("ft1")]
            b_zf = [Buf("zf0"), Buf("zf1")]
            b_gs = [Buf("gs0"), Buf("gs1")]
            DMA(P, "sp", cs2[:], C.d_cs2.rearrange("(c p) n -> p c n", p=128), w=[b_cs2], key="pe_cs2")
            DMA(P, "pool", fw[:], C.d_fourw[l].rearrange("(c p) n -> p c n", p=128), w=[b_fw], key="pe_fw")
            fin = FinWork(C, l)
            fin.alloc(s2)
            fin_next, fin_pending = 0, []
            blocks = [(lb * 256, "lat") for lb in range(8)] + ([(NL, "ctx")] if upd else [])
            dcv = C.d_dftc.rearrange("(lt p) n -> p lt n", p=128)
            dsv = C.d_dfts.rearrange("(lt p) n -> p lt n", p=128)

            def loadb(bi):
                t0, kind = blocks[bi]
                s = bi % 2
                if kind == "lat":
                    DMA(P, "sp", DC[s][:], dcv[:, :, t0:t0 + 256], w=[b_tab[s]], key="pe_dc%d" % s)
                    DMA(P, "sp", DS[s][:], dsv[:, :, t0:t0 + 256], w=[b_tabs[s]], key="pe_ds%d" % s)
                DMA(P, "sp", Zf[s][:], C.PT[R_ZF:R_ZF + 1024, t0:t0 + 256].rearrange("(m p) t -> p m t", p=128),
                    w=[b_zf[s]], key="pe_zf%d" % s)

            loadb(0)
            for bi, (t0, kind) in enumerate(blocks):
                if bi + 1 < len(blocks):
                    loadb(bi + 1)
                s = bi % 2
                scale = float((2048.0 * 256.0) ** -0.5) if kind == "lat" else float(1.0 / 256.0)
                for g in range(4):
                    for ct in range(2):
                        pk, bp, _ = ring.next()
                        if kind == "lat":
                            for lt in range(16):
                                MM(P, pk[:], UCS[:, lt, g, ct * 128:(ct + 1) * 128], DC[s][:, lt, :], lt == 0, False,
                                   [b_tab[s]], [bp])
                                MM(P, pk[:], UCS[:, lt, g, 256 + ct * 128:256 + (ct + 1) * 128], DS[s][:, lt, :], False, lt == 15,
                                   [b_tabs[s]], [bp])
                        else:
                            for lt in range(2):
                                MM(P, pk[:], UCS[:, 16 + lt, g, ct * 128:(ct + 1) * 128], cs2[:, lt, 0:256], lt == 0, False,
                                   [b_cs2], [bp])
                                MM(P, pk[:], UCS[:, 16 + lt, g, 256 + ct * 128:256 + (ct + 1) * 128], cs2[:, lt, 256:512], False, lt == 1,
                                   [b_cs2], [bp])
                        ACT(P, FT[s][:, g * 2 + ct, :], pk[:], AF.Copy, [bp], [b_ft[s]], scale=scale)
                for m in range(8):
                    pk, bp, _ = ring.next()
                    for cch in range(8):
                        MM(P, pk[:], fw[:, cch, m * 128:(m + 1) * 128], FT[s][:, cch, :], cch == 0, cch == 7,
                           [b_fw, b_ft[s]], [bp])
                    TT(P, "dve", gst[s][:, m, :], pk[:], Zf[s][:, m, :], ALU.mult, [bp, b_zf[s]], [b_gs[s]])
                DMA(P, "sp", C.GfT[:, t0:t0 + 256].rearrange("(m p) t -> p m t", p=128), gst[s][:], r=[b_gs[s]],
                    key="pe_gs%d" % s)
                for t in fin_pending:
                    fin.finish(t)
                fin_pending = []
                for _ in range(2):
                    if fin_next < fin.ntt:
                        fin.chain(fin_next)
                        fin_pending.append(fin_next)
                        fin_next += 1
            while fin_next < fin.ntt:
                fin.chain(fin_next)
                fin_pending.append(fin_next)
                fin_next += 1
            for t in fin_pending:
                fin.finish(t)
    P.fence()


def phase_ssd_prep(C, l):
    nc, P = C.nc, C.P
    with ExitStack() as st:
        cw = sb(st, nc, "f1_cw", [128, 16, 7], F32)
        cb = sb(st, nc, "f1_cb", [128, 16], F32)
        cosT = sb(st, nc, "f1_cos", [128, NL], F32)
        sinT = sb(st, nc, "f1_sin", [128, NL], F32)
        rot = sb(st, nc, "f1_rot", [128, 128], BF16)
        xin = [sb(st, nc, "f1_xin%d" % i, [128, NT], BF16) for i in range(2)]
        dg = [sb(st, nc, "f1_dg%d" % i, [128, 7, 128], BF16) for i in range(2)]
        xo = [sb(st, nc, "f1_xo%d" % i, [128, NT], BF16) for i in range(2)]
        xr = [sb(st, nc, "f1_xr%d" % i, [128, NT], BF16) for i in range(2)]
        t1 = [sb(st, nc, "f1_t1%d" % i, [128, 512], F32) for i in range(2)]
        t2 = [sb(st, nc, "f1_t2%d" % i, [128, 512], F32) for i in range(2)]
        tst = [sb(st, nc, "f1_ts%d" % i, [128, NTT, 128], BF16) for i in range(2)]
        rcv = Ring([ps(st, nc, "f1_pc%d" % i, [128, 512]) for i in range(3)], "f1pc")
        rrt = Ring([ps(st, nc, "f1_pr%d" % i, [128, 512]) for i in range(2)], "f1pr")
        rtr = Ring([ps(st, nc, "f1_pt%d" % i, [128, 8, 128], BF16) for i in range(2)], "f1pt")
        b_cw, b_cb, b_cs, b_rot = Buf("cw"), Buf("cb"), Buf("cs"), Buf("rot")
        b_sn = Buf("sn")
        b_xin = [Buf("xin0"), Buf("xin1")]
        b_dg = [Buf("dg0"), Buf("dg1")]
        b_xo = [Buf("xo0"), Buf("xo1")]
        b_xr = [Buf("xr0"), Buf("xr1")]
        b_t1 = [Buf("t10"), Buf("t11")]
        b_t2 = [Buf("t20"), Buf("t21")]
        b_ts = [Buf("ts0"), Buf("ts1")]
        DMA(P, "sp", cw[:], C.d_convw[l], w=[b_cw], key="f1_cw")
        DMA(P, "sp", cb[:], C.d_convb[l], w=[b_cb], key="f1_cb")
        DMA(P, "sp", cosT[:], C.d_ropec, w=[b_cs], key="f1_cos")
        DMA(P, "sp", sinT[:], C.d_ropes, w=[b_sn], key="f1_sin")
        DMA(P, "sp", rot[:], C.d_rot, w=[b_rot], key="f1_rot")

        def load(c):
            DMA(P, "sp", xin[c % 2][:], C.PT[R_XBC + c * 128:R_XBC + (c + 1) * 128, :], w=[b_xin[c % 2]], key="f1_xin%d" % (c % 2))

        segs = [(0, NL), (NL, NT)]
        cnt = {"tsi": 0, "ki": 0}
        srcs = {}

        def stage1(c):
            if c + 1 < 16:
                load(c + 1)
            s = c % 2
            for j in range(7):
                P.add("dve", lambda e, s=s, j=j, c=c: e.tensor_scalar_mul(out=dg[s][:, j, :], in0=C.ident[:], scalar1=cw[:, c, j:j + 1]),
                      [b_cw], [b_dg[s]])
            for (t0, tsz) in TBLK:
                lo_seg, hi_seg = segs[0] if t0 < NL else segs[1]
                pk, bp, _ = rcv.next()
                MM(P, pk[:, 0:tsz], dg[s][:, 3, :], xin[s][:, t0:t0 + tsz], True, False, [b_dg[s], b_xin[s]], [bp])
                for j in [0, 1, 2, 4, 5, 6]:
                    sh_ = j - 3
                    lo = max(t0 + sh_, lo_seg)
                    hi = min(t0 + tsz + sh_, hi_seg)
                    MM(P, pk[:, lo - sh_ - t0:hi - sh_ - t0], dg[s][:, j, :], xin[s][:, lo:hi], False, j == 6,
                       [b_dg[s], b_xin[s]], [bp])
                ACT(P, xo[s][:, t0:t0 + tsz], pk[:, 0:tsz], AF.Silu, [bp, b_cb], [b_xo[s]], bias=cb[:, c:c + 1])
            srcs[c] = (xo[s], b_xo[s])
            if c >= 8:
                for (t0, tsz) in TBLK[:4]:
                    pk, bp, _ = rrt.next()
                    k_ = cnt["ki"] % 2
                    cnt["ki"] += 1
                    MM(P, pk[:], rot[:], xo[s][:, t0:t0 + 512], True, True, [b_rot, b_xo[s]], [bp])
                    TT(P, "dve", t1[k_][:], xo[s][:, t0:t0 + 512], cosT[:, t0:t0 + 512], ALU.mult, [b_xo[s], b_cs], [b_t1[k_]])
                    TT(P, "dve", t2[k_][:], pk[:], sinT[:, t0:t0 + 512], ALU.mult, [bp, b_sn], [b_t2[k_]])
                    TT(P, "pool", xr[s][:, t0:t0 + 512], t1[k_][:], t2[k_][:], ALU.add, [b_t1[k_], b_t2[k_]], [b_xr[s]])
                CP(P, "pool", xr[s][:, NL:NT], xo[s][:, NL:NT], [b_xo[s]], [b_xr[s]])
                srcs[c] = (xr[s], b_xr[s])
                g = (c - 8) % 4
                dst = C.BT if c < 12 else C.CT
                DMA(P, "sp", dst[g * 128:(g + 1) * 128, :], xr[s][:], r=[b_xr[s]], key="f1_xr%d" % s)

        def stage2(c):
            if c >= 12:
                return
            src, bsrc = srcs[c]
            ts_ = cnt["tsi"] % 2
            cnt["tsi"] += 1
            for tg in range(3):
                n = min(8, NTT - tg * 8)
                pt, bpt, _ = rtr.next()
                for q in range(n):
                    t = tg * 8 + q
                    TR(P, pt[:, q, :], src[:, t * 128:(t + 1) * 128], C.ident[:], [bsrc], [bpt])
                CP(P, "dve", tst[ts_][:, tg * 8:tg * 8 + n, :], pt[:, 0:n, :], [bpt], [b_ts[ts_]])
            if c < 8:
                dv = C.XS.rearrange("(t p) n -> p t n", p=128)[:, :, c * 128:(c + 1) * 128]
            else:
                dv = C.Btm.rearrange("(t p) n -> p t n", p=128)[:, :, (c - 8) * 128:(c - 7) * 128]
            DMA(P, "sp", dv, tst[ts_][:], r=[b_ts[ts_]], key="f1_ts%d" % ts_)

        load(0)
        stage1(0)
        for c in range(16):
            if c + 1 < 16:
                stage1(c + 1)
            stage2(c)
    P.fence()


def phase_ssd_scan(C, l):
    nc, P = C.nc, C.P
    upd = C.upd[l]
    NLD, NPR, NU = 4, 3, 3
    with ExitStack() as st:
        masks = sb(st, nc, "f2_masks", [128, 4, 128], F32)
        alog = sb(st, nc, "f2_alog", [128, 32], F32)
        aneg = sb(st, nc, "f2_aneg", [128, 32], F32)
        ones32 = sb(st, nc, "f2_ones", [128, 128], F32)
        S32 = [[sb(st, nc, "f2_S%d%d" % (d, g), [128, 256], F32) for g in range(4)] for d in range(2)]
        Sbf = [[sb(st, nc, "f2_Sb%d%d" % (d, g), [128, 256], BF16) for g in range(4)] for d in range(2)]
        dtc = [sb(st, nc, "f2_dt%d" % i, [128, 32], F32) for i in range(NLD)]
        xsc = [sb(st, nc, "f2_xs%d" % i, [128, 16, 64], BF16) for i in range(NLD)]
        bc = [sb(st, nc, "f2_b%d" % i, [128, 512], BF16) for i in range(NLD)]
        btc = [sb(st, nc, "f2_bt%d" % i, [128, 4, 128], BF16) for i in range(NLD)]
        ctc = [sb(st, nc, "f2_ct%d" % i, [128, 4, 128], BF16) for i in range(NLD)]
        A = [sb(st, nc, "f2_A%d" % i, [128, 16], F32) for i in range(NPR)]
        ex = [sb(st, nc, "f2_ex%d" % i, [128, 3, 16], F32) for i in range(NPR)]
        xdt = [sb(st, nc, "f2_xdt%d" % i, [128, 16, 64], BF16) for i in range(NPR)]
        xdtd = [sb(st, nc, "f2_xdd%d" % i, [128, 16, 64], BF16) for i in range(NPR)]
        rhsA = [sb(st, nc, "f2_rA%d" % i, [128, 16, 128], F32) for i in range(NPR)]
        expD = [sb(st, nc, "f2_eD%d" % i, [128, 4, 128], BF16) for i in range(NU)]
        Gm = [sb(st, nc, "f2_Gm%d" % i, [128, 128], BF16) for i in range(NU)]
        MT = [sb(st, nc, "f2_MT%d" % i, [128, 4, 128], BF16) for i in range(NU)]
        ytmp = [sb(st, nc, "f2_yt%d" % i, [128, 4, 64], F32) for i in range(2)]
        ysum = [sb(st, nc, "f2_ys%d" % i, [128, 1024], F32) for i in range(2)]
        pD = [ps(st, nc, "f2_pD%d" % i, [128, 512]) for i in range(2)]
        pG = [ps(st, nc, "f2_pG%d" % i, [128, 128]) for i in range(2)]
        pY = [ps(st, nc, "f2_pY%d" % i, [128, 512]) for i in range(2)]
        pS = ps(st, nc, "f2_pS", [128, 256])
        psm = ps(st, nc, "f2_psm", [128, 3, 16])
        mk = lambda n, k: [Buf("%s%d" % (n, i)) for i in range(k)]
        b_pD, b_pG, b_pY = mk("pD", 2), mk("pG", 2), mk("pY", 2)
        b_pS, b_psm = Buf("pS"), Buf("psm")
        b_masks, b_a, b_ones = Buf("masks"), Buf("aneg"), Buf("ones32")
        b_S32 = [[Buf("S32%d%d" % (d, g)) for g in range(4)] for d in range(2)]
        b_Sbf = [[Buf("Sbf%d%d" % (d, g)) for g in range(4)] for d in range(2)]
        b_dt, b_xs, b_b, b_bt, b_ct = mk("dt", NLD), mk("xs", NLD), mk("b", NLD), mk("bt", NLD), mk("ct", NLD)
        b_A, b_ex, b_xdt, b_xdd, b_rA = mk("A", NPR), mk("ex", NPR), mk("xdt", NPR), mk("xdd", NPR), mk("rA", NPR)
        b_eD, b_Gm, b_MT = mk("eD", NU), mk("Gm", NU), mk("MT", NU)
        b_yt, b_ys = mk("yt", 2), mk("ys", 2)

        DMA(P, "sp", masks[:], C.d_masks, w=[b_masks], key="f2_masks")
        DMA(P, "sp", alog[:], C.d_alog[l].partition_broadcast(128), w=[b_a], key="f2_alog")
        ACT(P, aneg[:], alog[:], AF.Exp, [b_a], [b_a])
        P.add("dve", lambda e: e.tensor_scalar_mul(out=aneg[:], in0=aneg[:], scalar1=-1.0), [b_a], [b_a])
        MEMSET(P, "dve", ones32[:], 1.0, [b_ones])
        for d in range(2):
            for g in range(4):
                MEMSET(P, "dve", S32[d][g][:], 0.0, [b_S32[d][g]])
                MEMSET(P, "pool", Sbf[d][g][:], 0.0, [b_Sbf[d][g]])
        order = [[16, 17] + list(range(16)), [17, 16] + list(range(15, -1, -1))]
        steps = []
        for i in range(NTT):
            steps.append((0, order[0][i]))
            steps.append((1, order[1][i]))
        nsteps = len(steps)
        xsv = C.XS.rearrange("t (h p) -> t h p", p=64)

        def need_y(k):
            return (steps[k][1] < 16) or upd

        def load(k):
            d, ch = steps[k]
            s = k % NLD
            r0 = ch * 128
            DMA(P, "sp", dtc[s][:], C.TM_dt[r0:r0 + 128, :], w=[b_dt[s]], key="f2_dt%d" % s)
            DMA(P, "sp", xsc[s][:], xsv[r0:r0 + 128], w=[b_xs[s]], key="f2_xs%d" % s)
            DMA(P, "sp", bc[s][:], C.Btm[r0:r0 + 128, :], w=[b_b[s]], key="f2_b%d" % s)
            DMA(P, "sp", btc[s][:], C.BT[:, r0:r0 + 128].rearrange("(g n) s -> n g s", n=128), w=[b_bt[s]], key="f2_bt%d" % s)
            DMA(P, "sp", ctc[s][:], C.CT[:, r0:r0 + 128].rearrange("(g n) s -> n g s", n=128), w=[b_ct[s]], key="f2_ct%d" % s)

        def mset(d):
            return (0, 1, 1) if d == 0 else (2, 3, 3)

        def stepprep(k):
            d, ch = steps[k]
            sl, s = k % NLD, k % NPR
            m_inc, m_rest, m_D = mset(d)
            dsl = slice(d * 16, (d + 1) * 16)
            TT(P, "dve", A[s][:], dtc[sl][:, dsl], aneg[:, dsl], ALU.mult, [b_dt[sl], b_a], [b_A[s]])
            MM(P, psm[:, 0, :], masks[:, m_inc, :], A[s][:], True, True, [b_masks, b_A[s]], [b_psm])
            MM(P, psm[:, 1, :], masks[:, m_rest, :], A[s][:], True, True, [b_masks, b_A[s]], [b_psm])
            MM(P, psm[:, 2, :], ones32[:], A[s][:], True, True, [b_ones, b_A[s]], [b_psm])
            ACT(P, ex[s][:], psm[:], AF.Exp, [b_psm], [b_ex[s]])
            TT(P, "dve", xdt[s][:], xsc[sl][:], dtc[sl][:, dsl].unsqueeze(2).to_broadcast([128, 16, 64]), ALU.mult,
               [b_xs[sl], b_dt[sl]], [b_xdt[s]])
            TT(P, "pool", xdtd[s][:], xdt[s][:], ex[s][:, 1, :].unsqueeze(2).to_broadcast([128, 16, 64]), ALU.mult,
               [b_xdt[s], b_ex[s]], [b_xdd[s]])
            if need_y(k):
                for h in range(16):
                    ACT(P, rhsA[s][:, h, :], masks[:, m_inc, :], AF.Copy, [b_masks, b_A[s]], [b_rA[s]], scale=A[s][:, h:h + 1])

        items = [(k, g) for k in range(nsteps) for g in range(4)]
        N = len(items)

        def info(n):
            k, g = items[n]
            d, ch = steps[k]
            return k, g, d, ch, k % NLD, k % NPR, n % NU, n % 2, slice(4 * g, 4 * g + 4)

        def A0(n):
            k, g, d, ch, sl, s, u, p2, hs = info(n)
            if not need_y(k):
                return
            m_inc, m_rest, m_D = mset(d)
            MM(P, pD[p2][:], masks[:, m_D, :], rhsA[s][:, hs, :], True, True, [b_masks, b_rA[s]], [b_pD[p2]])
            MM(P, pG[p2][:], btc[sl][:, g, :], ctc[sl][:, g, :], True, True, [b_bt[sl], b_ct[sl]], [b_pG[p2]])

        def A1(n):
            k, g, d, ch, sl, s, u, p2, hs = info(n)
            if not need_y(k):
                return
            m_inc, m_rest, m_D = mset(d)
            ACT(P, expD[u][:], pD[p2][:].rearrange("p (h l) -> p h l", h=4), AF.Exp, [b_pD[p2]], [b_eD[u]])
            TT(P, "dve", Gm[u][:], pG[p2][:], masks[:, m_inc, :], ALU.mult, [b_pG[p2], b_masks], [b_Gm[u]])

        def A2(n):
            k, g, d, ch, sl, s, u, p2, hs = info(n)
            if not need_y(k):
                return
            TT(P, "dve", MT[u][:], expD[u][:], Gm[u][:].unsqueeze(1).to_broadcast([128, 4, 128]), ALU.mult,
               [b_eD[u], b_Gm[u]], [b_MT[u]])

        def B0(n):
            k, g, d, ch, sl, s, u, p2, hs = info(n)
            if need_y(k):
                for hh in range(4):
                    MM(P, pY[p2][:, hh * 64:(hh + 1) * 64], MT[u][:, hh, :], xdt[s][:, 4 * g + hh, :], True, True,
                       [b_MT[u], b_xdt[s]], [b_pY[p2]])
                MM(P, pY[p2][:, 256:512], ctc[sl][:, g, :], Sbf[d][g][:], True, True, [b_ct[sl], b_Sbf[d][g]], [b_pY[p2]])
            MM(P, pS[:], bc[sl][:, g * 128:(g + 1) * 128], xdtd[s][:, hs, :], True, True, [b_b[sl], b_xdd[s]], [b_pS])

        def B1(n):
            k, g, d, ch, sl, s, u, p2, hs = info(n)
            S3 = S32[d][g][:].rearrange("p (h q) -> p h q", h=4)
            TT(P, "dve", S3, S3, ex[s][:, 2, hs].unsqueeze(2).to_broadcast([128, 4, 64]), ALU.mult,
               [b_S32[d][g], b_ex[s]], [b_S32[d][g]])
            TT(P, "dve", S32[d][g][:], pS[:], S32[d][g][:], ALU.add, [b_pS, b_S32[d][g]], [b_S32[d][g]])
            ACT(P, Sbf[d][g][:], S32[d][g][:], AF.Copy, [b_S32[d][g]], [b_Sbf[d][g]])
            if need_y(k):
                ys_ = k % 2
                TT(P, "dve", ytmp[p2][:], pY[p2][:, 256:512].rearrange("p (h q) -> p h q", h=4),
                   ex[s][:, 0, hs].unsqueeze(2).to_broadcast([128, 4, 64]), ALU.mult, [b_pY[p2], b_ex[s]], [b_yt[p2]])
                TT(P, "dve", ysum[ys_][:, g * 256:(g + 1) * 256], pY[p2][:, 0:256], ytmp[p2][:].rearrange("p h q -> p (h q)"),
                   ALU.add, [b_pY[p2], b_yt[p2]], [b_ys[ys_]])
                if g == 3:
                    DMA(P, "sp", C.Y[d, ch * 128:(ch + 1) * 128, :], ysum[ys_][:], r=[b_ys[ys_]], key="f2_ys%d" % ys_)

        load(0)
        load(1)
        stepprep(0)
        for i in range(N + 4):
            if i < N:
                k, g = items[i]
                if g == 0:
                    if k + 2 < nsteps:
                        load(k + 2)
                    if k + 1 < nsteps:
                        stepprep(k + 1)
                A0(i)
            if 0 <= i - 1 < N:
                A1(i - 1)
            if 0 <= i - 2 < N:
                A2(i - 2)
            if 0 <= i - 4 < N:
                B1(i - 4)
            if 0 <= i - 3 < N:
                B0(i - 3)
    P.fence()


class FinWork:
    def __init__(self, C, l):
        self.C, self.l = C, l
        self.ntt = NTT if C.upd[l] else 16

    def alloc(self, st):
        C, nc, P, l = self.C, self.C.nc, self.C.P, self.l
        self.dfull = sb(st, nc, "f3_dfull", [128, 1024], F32)
        self.snw = sb(st, nc, "f3_snw", [128, 8], F32)
        self.y0 = [sb(st, nc, "f3_y0%d" % i, [128, 1024], F32) for i in range(2)]
        self.y1 = [sb(st, nc, "f3_y1%d" % i, [128, 1024], F32) for i in range(1)]
        self.xs = [sb(st, nc, "f3_xs%d" % i, [128, 1024], BF16) for i in range(2)]
        self.zs = [sb(st, nc, "f3_zs%d" % i, [128, 1024], BF16) for i in range(2)]
        self.tx = [sb(st, nc, "f3_tx%d" % i, [128, 1024], F32) for i in range(1)]
        self.ss = [sb(st, nc, "f3_ss%d" % i, [128, 1], F32) for i in range(2)]
        self.rs = [sb(st, nc, "f3_rs%d" % i, [128, 1], F32) for i in range(2)]
        self.yn = [sb(st, nc, "f3_yn%d" % i, [128, 1024], BF16) for i in range(4)]
        self.gst = [sb(st, nc, "f3_gs%d" % i, [128, 8, 128], BF16) for i in range(2)]
        self.ptr = [ps(st, nc, "f3_pt%d" % i, [128, 8, 128], BF16) for i in range(2)]
        mk = lambda n, k: [Buf("%s%d" % (n, i)) for i in range(k)]
        self.b_y0, self.b_y1, self.b_xs, self.b_zs, self.b_tx = mk("y0", 2), mk("y1", 1), mk("xs", 2), mk("zs", 2), mk("tx", 1)
        self.b_ss, self.b_rs, self.b_yn, self.b_gs, self.b_pt = mk("ss", 2), mk("rs", 2), mk("yn", 4), mk("gs", 2), mk("pt", 2)
        self.b_df, self.b_snw = Buf("dfull"), Buf("snw")
        DMA(P, "sp", self.dfull[:], C.d_dfull[l].partition_broadcast(128), w=[self.b_df], key="f3_dfull")
        DMA(P, "sp", self.snw[:], C.d_snormT[l], w=[self.b_snw], key="f3_snw")
        self.load(0)
        self.load_y1(0)

    def load(self, t):
        C, P = self.C, self.C.P
        s = t % 2
        r0 = t * 128
        DMA(P, "sp", self.y0[s][:], C.Y[0, r0:r0 + 128, :], w=[self.b_y0[s]], key="f3_y0%d" % s)
        DMA(P, "sp", self.xs[s][:], C.XS[r0:r0 + 128, :], w=[self.b_xs[s]], key="f3_xs%d" % s)
        DMA(P, "sp", self.zs[s][:], C.TM_zs[r0:r0 + 128, :], w=[self.b_zs[s]], key="f3_zs%d" % s)

    def load_y1(self, t):
        C, P = self.C, self.C.P
        r0 = t * 128
        DMA(P, "sp", self.y1[0][:], C.Y[1, r0:r0 + 128, :], w=[self.b_y1[0]], key="f3_y10")

    def chain(self, t):
        C, P = self.C, self.C.P
        if t + 1 < self.ntt:
            self.load(t + 1)
        s, q = t % 2, t % 4
        y0, y1, xs, zs, tx, yn = self.y0[s], self.y1[0], self.xs[s], self.zs[s], self.tx[0], self.yn[q]
        by0, by1, bxs, bzs, btx, byn = self.b_y0[s], self.b_y1[0], self.b_xs[s], self.b_zs[s], self.b_tx[0], self.b_yn[q]
        TT(P, "pool", tx[:], xs[:], self.dfull[:], ALU.mult, [bxs, self.b_df], [btx])
        TT(P, "dve", y0[:], y0[:], y1[:], ALU.add, [by0, by1], [by0])
        if t + 1 < self.ntt:
            self.load_y1(t + 1)
        TT(P, "dve", y0[:], y0[:], tx[:], ALU.add, [by0, btx], [by0])
        TT(P, "dve", y0[:], y0[:], zs[:], ALU.mult, [by0, bzs], [by0])
        ACT(P, yn[:], y0[:], AF.Square, [by0], [byn, self.b_ss[s]], accum_out=self.ss[s][:])
        ACT(P, self.rs[s][:], self.ss[s][:], AF.Sqrt, [self.b_ss[s]], [self.b_rs[s]], scale=1.0 / 1024, bias=EPS)
        RECIP(P, self.rs[s][:], self.rs[s][:], [self.b_rs[s]], [self.b_rs[s]])
        rs = self.rs[s]
        P.add("dve", lambda e: e.tensor_scalar_mul(out=yn[:], in0=y0[:], scalar1=rs[:, 0:1]), [by0, self.b_rs[s]], [byn])

    def finish(self, t):
        C, P = self.C, self.C.P
        s, q = t % 2, t % 4
        yn, ptr, gst = self.yn[q], self.ptr[s], self.gst[s]
        for m in range(8):
            TR(P, ptr[:, m, :], yn[:, m * 128:(m + 1) * 128], C.ident[:], [self.b_yn[q]], [self.b_pt[s]])
        for m in range(8):
            ACT(P, gst[:, m, :], ptr[:, m, :], AF.Copy, [self.b_pt[s], self.b_snw], [self.b_gs[s]], scale=self.snw[:, m:m + 1])
        DMA(P, "sp", C.GsT[:, t * 128:(t + 1) * 128].rearrange("(m p) t -> p m t", p=128), gst[:], r=[self.b_gs[s]],
            key="f3_gs%d" % s)


def phase_merge(C, l):
    nc, P = C.nc, C.P
    upd = C.upd[l]
    last = (l == DEPTH - 1)
    tblk = TBLK if upd else TBLK[:4]
    ntt = NTT if upd else 16
    with ExitStack() as st:
        mixT = sb(st, nc, "g_mix", [128, 16, NT], BF16)
        with ExitStack() as s1:
            wbs = [sb(s1, nc, "g_wb%d" % i, [128, 8, DM], BF16) for i in range(2)]
            Gx = [sb(s1, nc, "g_gx%d" % i, [128, 8, 512], BF16) for i in range(2)]
            gate = [sb(s1, nc, "g_gate%d" % i, [128, 16, 512], BF16) for i in range(2)]
            tmp = [sb(s1, nc, "g_tmp%d" % i, [128, 512], F32) for i in range(3)]
            ring = Ring([ps(s1, nc, "g_p1_%d" % i, [128, 512]) for i in range(6)], "g1")
            b_wbs = [Buf("wb0"), Buf("wb1")]
            b_mixt = [Buf("mix%d" % i) for i in range(16)]
            b_gx = [Buf("gx0"), Buf("gx1")]
            b_gate = [Buf("gate0"), Buf("gate1")]
            b_tmp = [Buf("tmp%d" % i) for i in range(3)]
            branches = [(C.d_wbna, C.GaT, R_GNA), (C.d_wbfour, C.GfT, R_GF), (C.d_wbssd, C.GsT, R_GS)]
            li = 0
            ki = 0
            def loadwb(bi):
                DMA(P, "pool", wbs[bi % 2][:], branches[bi][0][l].rearrange("(c p) n -> p c n", p=128), w=[b_wbs[bi % 2]],
                    key="g_wb%d" % (bi % 2))

            loadwb(0)
            for bi, (dw, GT_, rg) in enumerate(branches):
                if bi + 1 < len(branches):
                    loadwb(bi + 1)
                wb, b_wb = wbs[bi % 2], b_wbs[bi % 2]

                def loadblk(k, GT_=GT_, rg=rg):
                    t0, tsz = tblk[k]
                    s_ = (li + k) % 2
                    DMA(P, "sp", Gx[s_][:, :, 0:tsz], GT_[:, t0:t0 + tsz].rearrange("(c p) t -> p c t", p=128), w=[b_gx[s_]],
                        key="g_gx%d" % s_)
                    DMA(P, "sp", gate[s_][:, :, 0:tsz], C.PT[rg:rg + DM, t0:t0 + tsz].rearrange("(c p) t -> p c t", p=128),
                        w=[b_gate[s_]], key="g_gate%d" % s_)

                loadblk(0)
                for k, (t0, tsz) in enumerate(tblk):
                    if k + 1 < len(tblk):
                        loadblk(k + 1)
                    s_ = (li + k) % 2
                    for dt_ in range(16):
                        pk, bp, _ = ring.next()
                        for c in range(8):
                            MM(P, pk[:, 0:tsz], wb[:, c, dt_ * 128:(dt_ + 1) * 128], Gx[s_][:, c, 0:tsz], c == 0, c == 7,
                               [b_wb, b_gx[s_]], [bp])
                        dst = mixT[:, dt_, t0:t0 + tsz]
                        if bi == 0:
                            TT(P, "dve", dst, pk[:, 0:tsz], gate[s_][:, dt_, 0:tsz], ALU.mult, [bp, b_gate[s_]], [b_mixt[dt_]])
                        else:
                            k_ = ki % 3
                            ki += 1
                            TT(P, "dve", tmp[k_][:, 0:tsz], pk[:, 0:tsz], gate[s_][:, dt_, 0:tsz], ALU.mult,
                               [bp, b_gate[s_]], [b_tmp[k_]])
                            TT(P, "pool" if dt_ % 2 else "dve", dst, dst, tmp[k_][:, 0:tsz], ALU.add,
                               [b_tmp[k_], b_mixt[dt_]], [b_mixt[dt_]])
                li += len(tblk)
        P.fence()
        with ExitStack() as s2:
            wo = sb(s2, nc, "g_wo", [128, 16, DM], BF16)
            xt = [sb(s2, nc, "g_xt%d" % i, [128, DM], F32) for i in range(2)]
            tmp = [sb(s2, nc, "g_t2%d" % i, [128, 512], F32) for i in range(2)]
            fnw = sb(s2, nc, "g_fnw", [128, DM], F32)
            junk = sb(s2, nc, "g_junk", [128, DM], BF16)
            ss = [sb(s2, nc, "g_ss%d" % i, [128, 1], F32) for i in range(2)]
            rs = [sb(s2, nc, "g_rs%d" % i, [128, 1], F32) for i in range(2)]
            ring = Ring([ps(s2, nc, "g_p2_%d" % i, [128, 512]) for i in range(6)], "g2")
            b_wo, b_fnw, b_junk = Buf("wo"), Buf("fnw"), Buf("junk")
            b_xt = [Buf("xt0"), Buf("xt1")]
            b_tmp = [Buf("t20"), Buf("t21")]
            b_ss = [Buf("ss0"), Buf("ss1")]
            b_rs = [Buf("rs0"), Buf("rs1")]
            DMA(P, "pool", wo[:], C.d_wout[l].rearrange("(c p) n -> p c n", p=128), w=[b_wo], key="g_wo")
            if last:
                DMA(P, "sp", fnw[:], C.d_fnorm.partition_broadcast(128), w=[b_fnw], key="g_fnw")

            def src_of(t):
                return C.xl_src[l][t * 128:(t + 1) * 128, :] if t < 16 else C.xc_src[l][(t - 16) * 128:(t - 15) * 128, :]

            def load(t):
                DMA(P, "sp", xt[t % 2][:], src_of(t), w=[b_xt[t % 2]], key="g_xt%d" % (t % 2))

            load(0)
            ki = 0
            for t in range(ntt):
                if t + 1 < ntt:
                    load(t + 1)
                s_ = t % 2
                gb = C.glb if t < 16 else C.gcb
                for db in range(4):
                    pk, bp, _ = ring.next()
                    for c in range(16):
                        MM(P, pk[:], mixT[:, c, t * 128:(t + 1) * 128], wo[:, c, db * 512:(db + 1) * 512], c == 0, c == 15,
                           [b_wo], [bp])
                    k_ = ki % 2
                    ki += 1
                    TT(P, "dve", tmp[k_][:], pk[:], gb[:, db * 512:(db + 1) * 512], ALU.mult, [bp, C.b_mod], [b_tmp[k_]])
                    TT(P, "pool", xt[s_][:, db * 512:(db + 1) * 512], xt[s_][:, db * 512:(db + 1) * 512], tmp[k_][:], ALU.add,
                       [b_tmp[k_], b_xt[s_]], [b_xt[s_]])
                if not last:
                    dst = C.XL[t * 128:(t + 1) * 128, :] if t < 16 else C.XC[(t - 16) * 128:(t - 15) * 128, :]
                    DMA(P, "sp", dst, xt[s_][:], r=[b_xt[s_]], key="g_xo%d" % s_)
                else:
                    ACT(P, junk[:], xt[s_][:], AF.Square, [b_xt[s_]], [b_junk, b_ss[s_]], accum_out=ss[s_][:])
                    ACT(P, rs[s_][:], ss[s_][:], AF.Sqrt, [b_ss[s_]], [b_rs[s_]], scale=1.0 / DM, bias=EPS)
                    RECIP(P, rs[s_][:], rs[s_][:], [b_rs[s_]], [b_rs[s_]])
                    STT(P, "dve", xt[s_][:], xt[s_][:], rs[s_][:, 0:1], fnw[:], ALU.mult, ALU.mult,
                        [b_xt[s_], b_rs[s_], b_fnw], [b_xt[s_]])
                    DMA(P, "sp", C.d_out[t * 128:(t + 1) * 128, :], xt[s_][:], r=[b_xt[s_]], key="g_xo%d" % s_)
    P.fence()


LATER_PHASES.extend([("attn", phase_attn), ("ssdprep", phase_ssd_prep), ("ssdscan", phase_ssd_scan),
                     ("four", phase_four), ("merge", phase_merge)])


def _bf16(a):
    import ml_dtypes
    return np.ascontiguousarray(a).astype(ml_dtypes.bfloat16)


def _consts():
    k = np.arange(2048, dtype=np.int64)
    m = (k[:, None] * k[None, :]) % 2048
    ang = 2.0 * np.pi * m.astype(np.float64) / 2048.0
    dftc = _bf16(np.cos(ang))
    dfts = _bf16(np.sin(ang))
    k2 = np.arange(256, dtype=np.int64)
    a2 = 2.0 * np.pi * ((k2[:, None] * k2[None, :]) % 256).astype(np.float64) / 256.0
    cs1 = _bf16(np.concatenate([np.cos(a2), -np.sin(a2)], axis=1))
    cs2 = _bf16(np.concatenate([np.cos(a2), np.sin(a2)], axis=1))
    pos = np.arange(NL)
    inv = (np.float32(10000.0) ** (-np.arange(32, dtype=np.float32) / np.float32(32))).astype(np.float32)
    row = (pos // 64).astype(np.float32)
    col = (pos % 64).astype(np.float32)
    ropec = np.zeros((128, NL), np.float32)
    ropes = np.zeros((128, NL), np.float32)
    for n in range(128):
        p = row if n < 64 else col
        a = (p * inv[n % 32]).astype(np.float32)
        ropec[n] = np.cos(a).astype(np.float32)
        ropes[n] = np.sin(a).astype(np.float32)
    rot = np.zeros((128, 128), np.float32)
    for mm_ in range(128):
        if (mm_ % 64) < 32:
            rot[mm_ + 32, mm_] = -1.0
        else:
            rot[mm_ - 32, mm_] = 1.0
    kk = np.arange(128)
    le = (kk[:, None] <= kk[None, :]).astype(np.float32)
    gt = (kk[:, None] > kk[None, :]).astype(np.float32)
    ge = (kk[:, None] >= kk[None, :]).astype(np.float32)
    lt = (kk[:, None] < kk[None, :]).astype(np.float32)
    masks = np.ascontiguousarray(np.stack([le, gt, ge, lt], axis=1))
    return dict(ident=_bf16(np.eye(128, dtype=np.float32)), dftc=dftc, dfts=dfts, cs1=cs1, cs2=cs2,
                ropec=ropec, ropes=ropes, rot=_bf16(rot), masks=masks)


NEG = -80.0


def _expand_rpb(rpb):
    GW, WR, WC = 64, 8, 16
    rows = NL // GW
    qc = np.arange(GW)
    qstart = np.clip(qc - WC // 2, 0, GW - WC)
    kc = np.arange(GW)
    colmask = (kc[:, None] >= qstart[None, :]) & (kc[:, None] < qstart[None, :] + WC)
    dc = np.clip(kc[:, None] - qc[None, :] + WC - 1, 0, 2 * WC - 2)
    combos = [(8, 8 + d) for d in (-2, -1, 0, 1, 2)]
    for i in (0, 1):
        combos += [(i, a) for a in range(4)]
    for i in (14, 15):
        combos += [(i, a) for a in range(12, 16)]
    out = np.full((8, 128, NBT, 128), NEG, np.float32)
    for ti, (i, a) in enumerate(combos):
        for kr_ in range(2):
            kr = 2 * a + kr_
            for r_ in range(2):
                r = 2 * i + r_
                start = min(max(r - WR // 2, 0), rows - WR)
                if not (start <= kr < start + WR):
                    continue
                dr = kr - r + WR - 1
                blk = np.where(colmask[None], rpb[:, dr][:, dc], np.float32(NEG))
                out[:, kr_ * 64:(kr_ + 1) * 64, ti, r_ * 64:(r_ + 1) * 64] = blk
    return out


def prep_inputs(inp):
    f = lambda a: np.ascontiguousarray(a, dtype=np.float32)
    sh = {}
    sh["w_ada"] = f(inp["w_ada"])
    sh["bada"] = f(inp["b_ada"]).reshape(DEPTH, 1, 6144)
    sh["badaT"] = f(inp["b_ada"].reshape(DEPTH, 48, 128).transpose(0, 2, 1))
    sh["normwT"] = f(inp["norm_w"].reshape(DEPTH, 16, 128).transpose(0, 2, 1))
    sh["w_in"] = f(inp["w_in"])
    sh["rpb"] = f(np.stack([_expand_rpb(inp["na_rpb"][l]) for l in range(DEPTH)]))
    for k in ("four_w", "wb_na", "wb_four", "wb_ssd", "w_out"):
        sh[k] = f(inp[k])
    sh["convw"] = f(inp["ssd_conv_w"].reshape(DEPTH, 7, 16, 128).transpose(0, 3, 2, 1))
    sh["convb"] = f(inp["ssd_conv_b"].reshape(DEPTH, 16, 128).transpose(0, 2, 1))
    sh["dtbias"] = f(inp["ssd_dt_bias"]).reshape(DEPTH, 1, 32)
    sh["alog"] = f(inp["ssd_a_log"]).reshape(DEPTH, 1, 32)
    sh["dfull"] = f(np.repeat(inp["ssd_d"], 64, axis=1)).reshape(DEPTH, 1, 1024)
    sh["snormT"] = f(inp["ssd_norm_w"].reshape(DEPTH, 8, 128).transpose(0, 2, 1))
    sh["fnorm"] = f(inp["final_norm_w"]).reshape(1, DM)
    sh.update(_consts())
    maps = []
    for b in range(inp["x"].shape[0]):
        m = dict(sh)
        m["x"] = f(inp["x"][b])
        m["ctx"] = f(inp["ctx"][b])
        m["cc"] = f(np.stack([inp["c"][b].reshape(16, 128).T, inp["c_ctx"].reshape(16, 128).T], axis=2))
        maps.append(m)
    return maps


def build(debug=False, stop_after=None):
    import ml_dtypes
    nc = bass.Bass("TRN2", target_bir_lowering=False)
    C = Ctx()
    C.nc = nc
    C.P = P = Prog(nc)
    C.upd = [True, False]
    C.later_phases = list(LATER_PHASES)

    def din(name, shape, dt=F32):
        return nc.dram_tensor(name, list(shape), dt, kind="ExternalInput").ap()

    def scr(name, shape, dt):
        return nc.dram_tensor(name, list(shape), dt, kind=("ExternalOutput" if debug else "Internal")).ap()

    C.d_x = din("x", [NL, DM])
    C.d_ctx = din("ctx", [NCX, DM])
    C.d_cc = din("cc", [128, 16, 2])
    C.d_wada = din("w_ada", [DEPTH, DM, 6144])
    C.d_bada = din("bada", [DEPTH, 1, 6144])
    C.d_badaT = din("badaT", [DEPTH, 128, 48])
    C.d_normwT = din("normwT", [DEPTH, 128, 16])
    C.d_win = din("w_in", [DEPTH, DM, INW])
    C.d_rpb = din("rpb", [DEPTH, 8, 128, NBT, 128])
    C.d_fourw = din("four_w", [DEPTH, 1024, 1024])
    C.d_wbna = din("wb_na", [DEPTH, 1024, DM])
    C.d_wbfour = din("wb_four", [DEPTH, 1024, DM])
    C.d_wbssd = din("wb_ssd", [DEPTH, 1024, DM])
    C.d_wout = din("w_out", [DEPTH, DM, DM])
    C.d_convw = din("convw", [DEPTH, 128, 16, 7])
    C.d_convb = din("convb", [DEPTH, 128, 16])
    C.d_dtbias = din("dtbias", [DEPTH, 1, 32])
    C.d_alog = din("alog", [DEPTH, 1, 32])
    C.d_dfull = din("dfull", [DEPTH, 1, 1024])
    C.d_snormT = din("snormT", [DEPTH, 128, 8])
    C.d_fnorm = din("fnorm", [1, DM])
    C.d_ident = din("ident", [128, 128], BF16)
    C.d_dftc = din("dftc", [2048, 2048], BF16)
    C.d_dfts = din("dfts", [2048, 2048], BF16)
    C.d_cs1 = din("cs1", [256, 512], BF16)
    C.d_cs2 = din("cs2", [256, 512], BF16)
    C.d_ropec = din("ropec", [128, NL])
    C.d_ropes = din("ropes", [128, NL])
    C.d_rot = din("rot", [128, 128], BF16)
    C.d_masks = din("masks", [128, 4, 128])
    C.d_out = nc.dram_tensor("out", [NL, DM], F32, kind="ExternalOutput").ap()

    C.XL = scr("XL", [NL, DM], F32)
    C.XC = scr("XC", [NCX, DM], F32)
    C.PT = scr("PT", [PT_ROWS, NT], BF16)
    C.TM_v = scr("TM_v", [NT, 1024], BF16)
    C.TM_zs = scr("TM_zs", [NT, 1024], BF16)
    C.TM_dt = scr("TM_dt", [NT, 32], F32)
    C.GaT = scr("GaT", [1024, NT], BF16)
    C.GfT = scr("GfT", [1024, NT], BF16)
    C.GsT = scr("GsT", [1024, NT], BF16)
    C.XS = scr("XS", [NT, 1024], BF16)
    C.Btm = scr("Btm", [NT, 512], BF16)
    C.BT = scr("BT", [512, NT], BF16)
    C.CT = scr("CT", [512, NT], BF16)
    C.Y = scr("Y", [2, NT, 1024], F32)
    C.xl_src = [C.d_x, C.XL]
    C.xc_src = [C.d_ctx, C.XC]

    phases = C.phases = []

    def done(name):
        phases.append(name)
        return stop_after is not None and name == stop_after

    with ExitStack() as top:
        C.ident = sb(top, nc, "ident", [128, 128], BF16)
        C.w1 = sb(top, nc, "w1", [128, 2, 16], F32)
        C.sh = sb(top, nc, "shv", [128, 2, 16], F32)
        C.glb = sb(top, nc, "glb", [128, DM], F32)
        C.gcb = sb(top, nc, "gcb", [128, DM], F32)
        C.b_mod = Buf("mod")
        DMA(P, "sp", C.ident[:], C.d_ident, key="ident")
        P.fence()
        stop = False
        for l in range(DEPTH):
            with ExitStack() as sth:
                C.hT = sb(sth, nc, "hT", [128, 16, NT], BF16)
                phase_modh(C, l)
                if done("h%d" % l):
                    break
                phase_gemm(C, l)
            if done("gemm%d" % l):
                break
            for name, fn in C.later_phases:
                fn(C, l)
                if done("%s%d" % (name, l)):
                    stop = True
                    break
            if stop:
                break
            P.new_epoch()
        P.emit(top)
    return nc, C


_CACHE = {}


def kernel(**inputs):
    maps = prep_inputs(inputs)
    if "nc" not in _CACHE:
        _CACHE["nc"] = build(debug=False)[0]
    nc = _CACHE["nc"]
    res = run_bass_kernel_spmd(nc, maps, core_ids=list(range(len(maps))))
    out = np.stack([np.asarray(r["out"], dtype=np.float32) for r in res.results], axis=0)
    return out
```
